# Optimizing a Trainium2 kernel written in Bass

```python
import jax, jax.numpy as jnp
from jax import lax
import numpy as np

D_MODEL = 2048
BATCH = 16
SEQ = 2048
DEPTH = 1
DEC_BATCH = 128
DEC_SEQ = 4
PAST_LEN = 16384
PAGE_SIZE = 128

HEAD_DIM = 64
ATTN_W = D_MODEL // 2
N_HEADS = ATTN_W // HEAD_DIM
N_KV_HEADS = N_HEADS // 4
GROUP = N_HEADS // N_KV_HEADS
KV_W = N_KV_HEADS * HEAD_DIM
CONV_CH = D_MODEL - ATTN_W
MIX_W = ATTN_W + CONV_CH
IN_W = ATTN_W + 2 * KV_W + 2 * CONV_CH
WINDOW = 128
BLOCK = 128
CACHE_WIN = min(WINDOW, PAST_LEN)
CONV_W = 31
CONV_BUF = min(CONV_W - 1, PAST_LEN)
D_FF = 4 * D_MODEL
EPS = 1e-6
ATTN_SCALE = HEAD_DIM ** -0.5
NEG = -1e30

kernel_name = "hymba_swa_sink_conformer_conv_sqrelu"


def _rmsnorm(x, g):
    xf = x.astype(jnp.float32)
    y = xf * lax.rsqrt(jnp.mean(xf * xf, axis=-1, keepdims=True) + EPS)
    return (y * g.astype(jnp.float32)).astype(x.dtype)


def _layernorm(x, g, b):
    xf = x.astype(jnp.float32)
    mu = jnp.mean(xf, axis=-1, keepdims=True)
    xc = xf - mu
    y = xc * lax.rsqrt(jnp.mean(xc * xc, axis=-1, keepdims=True) + EPS)
    return (y * g.astype(jnp.float32) + b.astype(jnp.float32)).astype(x.dtype)


def _project(h, w_in, q_norm, k_norm):
    z = h @ w_in
    q, k, v, a, gate = jnp.split(
        z, [ATTN_W, ATTN_W + KV_W, ATTN_W + 2 * KV_W, ATTN_W + 2 * KV_W + CONV_CH], axis=-1)
    lead = h.shape[:-1]
    q = _rmsnorm(q.reshape(*lead, N_KV_HEADS, GROUP, HEAD_DIM), q_norm)
    k = _rmsnorm(k.reshape(*lead, N_KV_HEADS, HEAD_DIM), k_norm)
    v = v.reshape(*lead, N_KV_HEADS, HEAD_DIM)
    u = a * jax.nn.sigmoid(gate)
    return q, k, v, u


def _sink_attn(q, k, v, q_pos, k_pos, sinks):
    s = jnp.einsum('...qhgd,...khd->...hgqk', q, k,
                   preferred_element_type=jnp.float32) * ATTN_SCALE
    kp = k_pos[..., None, :]
    qp = q_pos[..., :, None]
    mask = (kp <= qp) & (kp > qp - WINDOW) & (kp >= 0)
    s = jnp.where(mask[..., None, None, :, :], s, NEG)
    sink = sinks.astype(jnp.float32).reshape(N_KV_HEADS, GROUP, 1, 1)
    sink = jnp.broadcast_to(sink, s.shape[:-1] + (1,))
    p = jax.nn.softmax(jnp.concatenate([s, sink], axis=-1), axis=-1)[..., :-1]
    return jnp.einsum('...hgqk,...khd->...qhgd', p.astype(v.dtype), v)


def _conv_tail(u_ext, w_dw, b_dw, g_ln, b_ln):
    y = lax.conv_general_dilated(
        u_ext, w_dw[:, None, :].astype(u_ext.dtype), window_strides=(1,), padding='VALID',
        dimension_numbers=('NWC', 'WIO', 'NWC'), feature_group_count=CONV_CH)
    y = _layernorm(y + b_dw, g_ln, b_ln)
    return jax.nn.silu(y)


def _merge_and_mlp(x, attn_o, conv_o, w_out, g_mlp, w_up, w_down):
    x1 = x + jnp.concatenate([attn_o, conv_o], axis=-1) @ w_out
    h = _rmsnorm(x1, g_mlp)
    return x1 + jnp.square(jax.nn.relu(h @ w_up)) @ w_down


def setup_inputs(seed: int = 0) -> dict:
    key = jax.random.key(seed)
    ks = jax.random.split(key, 20)
    f = jnp.float32
    n = lambda k, shp, sc: (jax.random.normal(k, shp, f) * sc)
    return {
        "x_prompt": n(ks[0], (BATCH, SEQ, D_MODEL), 1.0),
        "x_sample": n(ks[1], (DEC_BATCH, DEC_SEQ, D_MODEL), 1.0),
        "cache_k": n(ks[2], (DEPTH, DEC_BATCH, CACHE_WIN, N_KV_HEADS, HEAD_DIM), 1.0),
        "cache_v": n(ks[3], (DEPTH, DEC_BATCH, CACHE_WIN, N_KV_HEADS, HEAD_DIM), 1.0),
        "state_conv": n(ks[4], (DEPTH, DEC_BATCH, CONV_BUF, CONV_CH), 0.5),
        "g_mix_norm": 1.0 + n(ks[5], (DEPTH, D_MODEL), 0.02),
        "w_in": n(ks[6], (DEPTH, D_MODEL, IN_W), D_MODEL ** -0.5),
        "q_norm": 1.0 + n(ks[7], (DEPTH, HEAD_DIM), 0.02),
        "k_norm": 1.0 + n(ks[8], (DEPTH, HEAD_DIM), 0.02),
        "sinks": n(ks[9], (DEPTH, N_HEADS), 0.5),
        "w_dw": n(ks[10], (DEPTH, CONV_W, CONV_CH), CONV_W ** -0.5),
        "b_dw": n(ks[11], (DEPTH, CONV_CH), 0.02),
        "g_conv_ln": 1.0 + n(ks[12], (DEPTH, CONV_CH), 0.02),
        "b_conv_ln": n(ks[13], (DEPTH, CONV_CH), 0.02),
        "w_out": n(ks[14], (DEPTH, MIX_W, D_MODEL), MIX_W ** -0.5),
        "g_mlp_norm": 1.0 + n(ks[15], (DEPTH, D_MODEL), 0.02),
        "w_up": n(ks[16], (DEPTH, D_MODEL, D_FF), D_MODEL ** -0.5),
        "w_down": n(ks[17], (DEPTH, D_FF, D_MODEL), D_FF ** -0.5),
    }


def reference(x_prompt, x_sample, cache_k, cache_v, state_conv,
              g_mix_norm, w_in, q_norm, k_norm, sinks, w_dw, b_dw, g_conv_ln, b_conv_ln,
              w_out, g_mlp_norm, w_up, w_down):
    B, S, _ = x_prompt.shape
    Bd, Sd, _ = x_sample.shape
    n_blk = S // BLOCK
    q_pos_p = jnp.arange(S, dtype=jnp.int32).reshape(n_blk, BLOCK)
    k_pos_p = q_pos_p[:, :1] - BLOCK + jnp.arange(2 * BLOCK, dtype=jnp.int32)
    q_pos_s = PAST_LEN + jnp.arange(Sd, dtype=jnp.int32)
    k_pos_s = PAST_LEN - CACHE_WIN + jnp.arange(CACHE_WIN + Sd, dtype=jnp.int32)

    xp, xs = x_prompt, x_sample
    nk_p, nv_p, nc_p, nk_s, nv_s, nc_s = [], [], [], [], [], []
    for l in range(DEPTH):
        h = _rmsnorm(xp, g_mix_norm[l])
        q, k, v, u = _project(h, w_in[l], q_norm[l], k_norm[l])
        qb = q.reshape(B, n_blk, BLOCK, N_KV_HEADS, GROUP, HEAD_DIM)
        kb = k.reshape(B, n_blk, BLOCK, N_KV_HEADS, HEAD_DIM)
        vb = v.reshape(B, n_blk, BLOCK, N_KV_HEADS, HEAD_DIM)
        pad = jnp.zeros_like(kb[:, :1])
        kk = jnp.concatenate([jnp.concatenate([pad, kb[:, :-1]], axis=1), kb], axis=2)
        vv = jnp.concatenate([jnp.concatenate([pad, vb[:, :-1]], axis=1), vb], axis=2)
        attn_p = _sink_attn(qb, kk, vv, q_pos_p, k_pos_p, sinks[l]).reshape(B, S, ATTN_W)
        u_ext = jnp.concatenate([jnp.zeros((B, CONV_W - 1, CONV_CH), u.dtype), u], axis=1)
        conv_p = _conv_tail(u_ext, w_dw[l], b_dw[l], g_conv_ln[l], b_conv_ln[l])
        xp = _merge_and_mlp(xp, attn_p, conv_p, w_out[l], g_mlp_norm[l], w_up[l], w_down[l])
        nk_p.append(k[:, S - CACHE_WIN:])
        nv_p.append(v[:, S - CACHE_WIN:])
        nc_p.append(u[:, S - CONV_BUF:])

        h = _rmsnorm(xs, g_mix_norm[l])
        q, k, v, u = _project(h, w_in[l], q_norm[l], k_norm[l])
        kk = jnp.concatenate([cache_k[l].astype(k.dtype), k], axis=1)
        vv = jnp.concatenate([cache_v[l].astype(v.dtype), v], axis=1)
        attn_s = _sink_attn(q, kk, vv, q_pos_s, k_pos_s, sinks[l]).reshape(Bd, Sd, ATTN_W)
        u_ext = jnp.concatenate([state_conv[l].astype(u.dtype), u], axis=1)
        conv_s = _conv_tail(u_ext[:, CONV_BUF - (CONV_W - 1):], w_dw[l], b_dw[l],
                            g_conv_ln[l], b_conv_ln[l])
        xs = _merge_and_mlp(xs, attn_s, conv_s, w_out[l], g_mlp_norm[l], w_up[l], w_down[l])
        nk_s.append(kk[:, Sd:])
        nv_s.append(vv[:, Sd:])
        nc_s.append(u_ext[:, Sd:])

    return (xp, xs, jnp.stack(nk_p), jnp.stack(nv_p), jnp.stack(nc_p),
            jnp.stack(nk_s), jnp.stack(nv_s), jnp.stack(nc_s))
```

```python
import numpy as np
from contextlib import ExitStack
import concourse.bass as bass
import concourse.mybir as mybir
from concourse.bass_utils import run_bass_kernel_spmd

F32 = mybir.dt.float32
BF16 = mybir.dt.bfloat16
AF = mybir.ActivationFunctionType
ALU = mybir.AluOpType

NCORES = 8
D = 2048
S = 2048
HD = 64
KVW = 256
CCH = 1024
INW = 3584
DFF = 8192
T = 512
CW = 31
CB = 30
NSEQ = 2
NSMP = 16
SD = 4
NS_TOK = NSMP * SD
EPS = 1e-6
NPANEL = 47
P_Q = 0
P_KV = 2
P_AG = [3, 4, 5, 6]
P_DG = [7, 8, 9, 10]
P_WO = 11
P_FF = 15
NSLOT = 3
PW = 16 * 512


class DSem:
    def __init__(self, sem):
        self.sem = sem
        self.cnt = 0


class Buf:
    def __init__(self, name=""):
        self.name = name
        self.w = {}
        self.r = {}


class Q:
    def __init__(self, name, eng, sem, is_pe=False):
        self.name = name
        self.eng = eng
        self.sem = sem
        self.cnt = 0
        self.seen = {}
        self.is_pe = is_pe

    def need(self, sem, val):
        if sem is self.sem or sem == self.sem:
            if self.is_pe:
                return
            if self.cnt + 1 - val >= 3:
                return
        if self.seen.get(sem, 0) >= val:
            return
        self.eng.wait_ge(sem, val)
        self.seen[sem] = val


def _merge(d, e):
    for k, v in e.items():
        if d.get(k, 0) < v:
            d[k] = v


def _deps(reads, writes):
    d = {}
    for b in reads:
        _merge(d, b.w)
    for b in writes:
        _merge(d, b.w)
        _merge(d, b.r)
    return d


def _record(ev_sem, ev_val, reads, writes):
    for b in reads:
        if b.r.get(ev_sem, 0) < ev_val:
            b.r[ev_sem] = ev_val
    for b in writes:
        if b.w.get(ev_sem, 0) < ev_val:
            b.w[ev_sem] = ev_val


def op(q, fn, reads=(), writes=()):
    for s, v in _deps(reads, writes).items():
        q.need(s, v)
    inst = fn()
    q.cnt += 1
    inst.then_inc(q.sem, 1)
    _record(q.sem, q.cnt, reads, writes)


def pe_group(q, fns, reads=(), writes=()):
    for s, v in _deps(reads, writes).items():
        q.need(s, v)
    inst = None
    for f in fns:
        inst = f()
    q.cnt += 1
    inst.then_inc(q.sem, 1)
    _record(q.sem, q.cnt, reads, writes)


def dma(q, dsem, out, in_, reads=(), writes=()):
    for s, v in _deps(reads, writes).items():
        q.need(s, v)
    inst = q.eng.dma_start(out=out, in_=in_)
    dsem.cnt += 16
    inst.then_inc(dsem.sem, 16)
    _record(dsem.sem, dsem.cnt, reads, writes)


_DBG = {"prompt": True, "sample": True, "stop": 99}


def build_program():
    nc = bass.Bass("TRN2", target_bir_lowering=False)
    es = ExitStack()

    def din(name, shape):
        return nc.dram_tensor(name, list(shape), F32, kind="ExternalInput").ap()

    def dout(name, shape):
        return nc.dram_tensor(name, list(shape), F32, kind="ExternalOutput").ap()

    xp = din("xp", [NSEQ * S, D])
    xs = din("xs", [NS_TOK, D])
    ck = din("ck", [NSMP, 128, KVW])
    cv = din("cv", [NSMP, 128, KVW])
    sc = din("sc", [NSMP, CB, CCH])
    g_mix = din("g_mix", [D])
    w_in = din("w_in", [D, INW])
    q_norm = din("q_norm", [HD])
    k_norm = din("k_norm", [HD])
    sinks = din("sinks", [16])
    w_dw = din("w_dw", [CW, CCH])
    b_dw = din("b_dw", [CCH])
    g_ln = din("g_ln", [CCH])
    b_ln = din("b_ln", [CCH])
    w_out = din("w_out", [D, D])
    g_mlp = din("g_mlp", [D])
    w_up = din("w_up", [D, DFF])
    w_down = din("w_down", [DFF, D])

    yp = dout("yp", [NSEQ * S, D])
    ys = dout("ys", [NS_TOK, D])
    nkp = dout("nkp", [NSEQ, 128, KVW])
    nvp = dout("nvp", [NSEQ, 128, KVW])
    ncp = dout("ncp", [NSEQ, CB, CCH])
    nks = dout("nks", [NSMP, 128, KVW])
    nvs = dout("nvs", [NSMP, 128, KVW])
    ncs = dout("ncs", [NSMP, CB, CCH])

    wscr = nc.dram_tensor("wscr", [NPANEL, 128, PW], BF16, kind="Internal").ap()
    dbg_out = nc.dram_tensor("dbg", [128, 16, NS_TOK], F32, kind="ExternalOutput").ap() if _DBG.get("dump") else None

    def sb(name, shape, dt):
        return es.enter_context(nc.sbuf_tensor(name, list(shape), dt))

    def newsem(name):
        return es.enter_context(nc.semaphore(name))

    PE = Q("pe", nc.tensor, newsem("s_pe"), is_pe=True)
    ACT = Q("act", nc.scalar, newsem("s_act"))
    DVE = Q("dve", nc.vector, newsem("s_dve"))
    POOL = Q("pool", nc.gpsimd, newsem("s_pool"))
    SP = Q("sp", nc.sync, newsem("s_sp"))

    out_sems = []

    def osem(name):
        d = DSem(newsem(name))
        out_sems.append(d)
        return d

    xin = sb("xin", [128, D], F32)
    xhat = sb("xhat", [128, D], BF16)
    XT = sb("XT", [128, 16, T], BF16)
    MH = sb("MH", [128, 16, T], BF16)
    qT = sb("qT", [128, 8, T], BF16)
    kT = sb("kT", [128, 2, 128 + T], BF16)
    vB = sb("vB", [128, 5, KVW], BF16)
    UY = sb("UY", [128, 8 * (CB + T + 2)], F32)
    UYb = UY[:, :].bitcast(BF16)
    uTb = UYb[:, 0:8 * (CB + T + 2)].rearrange("p (c t) -> p c t", c=8)
    ybf = UYb[:, 8 * (CB + T + 2):8 * (CB + T + 2) + 8 * T].rearrange("p (c t) -> p c t", c=8)
    uhalo = sb("uhalo", [128, 8, CB], F32)
    PH = sb("PH", [128, 2 * 4096], BF16)
    x1 = sb("x1", [128, 4, D], F32)
    W = sb("W", [128, NSLOT, PW], BF16)
    NT32 = 8
    t32 = sb("t32", [128, NT32, T], F32)
    NT16 = 4
    t16 = sb("t16", [128, NT16, T], BF16)
    io = sb("io", [128, 128], F32)
    ident_f = sb("ident_f", [128, 128], F32)
    ident_b = sb("ident_b", [128, 128], BF16)
    masks = sb("masks", [128, 2, 128], BF16)
    ones_b = sb("ones_b", [128, 64], BF16)
    blk1 = sb("blk1", [128, 128], BF16)
    onesN = sb("onesN", [128, 128], BF16)
    eps_c = sb("eps_c", [128, 1], F32)
    vrows = sb("vrows", [64, 128], F32)
    wrows = sb("wrows", [128, 2, 128], F32)
    cols = sb("cols", [128, 64], F32)
    wdwc = sb("wdwc", [128, 248], F32)
    es16 = sb("es16", [128, 16], F32)
    ES = sb("ES", [128, 2, 4, 128], F32)
    stat = sb("stat", [128, 8, 4], F32)
    Eblk = sb("Eblk", [16, 64], BF16)
    maskN = sb("maskN", [64, 64], BF16)
    maskCA = sb("maskCA", [128, NSMP, NS_TOK], BF16)
    print("sbuf bytes remaining", nc.sbuf_bytes_remaining)
    ostg = sb("ostg", [128, 2, 512], F32)
    kf = sb("kf", [128, 2, 128], F32)

    pf = es.enter_context(nc.psum_tensor("pf", [128, 6, 512], F32))
    trp = es.enter_context(nc.psum_tensor("trp", [128, 16, 128], BF16))
    trf = trp[:, :, :].rearrange("p a b -> p (a b)").bitcast(F32).rearrange("p (k n) -> p k n", k=2)

    B = {}

    def bf(name):
        if name not in B:
            B[name] = Buf(name)
        return B[name]

    bank = [bf(f"bank{i}") for i in range(6)]
    b_tr = bf("trp")
    b_W = [bf(f"W{i}") for i in range(NSLOT)]
    b_panel = [bf(f"panel{i}") for i in range(NPANEL)]
    b_x1 = [bf(f"x1_{i}") for i in range(4)]
    b_t32 = [bf(f"t32_{i}") for i in range(NT32)]
    b_t16 = [bf(f"t16_{i}") for i in range(NT16)]
    b_u = [bf(f"u{i}") for i in range(8)]
    b_yc = [bf(f"y{i}") for i in range(8)]
    b_stat = [bf(f"stat{i}") for i in range(8)]
    b_ostg = [bf("ostg0"), bf("ostg1")]
    b_const = bf("const")

    rr = {"t32": 0, "t16": 0, "mm": 0, "stat": 0, "ostg": 0, "osl": 0, "mm4": 0}

    def get_t32():
        i = rr["t32"] % (NT32 - 2)
        rr["t32"] += 1
        return t32[:, i, :], b_t32[i]

    def get_t16():
        i = rr["t16"] % NT16
        rr["t16"] += 1
        return t16[:, i, :], b_t16[i]

    def get_mm():
        i = rr["mm"] % 2
        rr["mm"] += 1
        return pf[:, i, :], bank[i]

    def get_mm4():
        i = rr["mm4"] % 4
        rr["mm4"] += 1
        return pf[:, i, :], bank[i]

    def get_stat():
        i = rr["stat"] % 8
        rr["stat"] += 1
        return stat[:, i, :], b_stat[i]

    def get_ostg():
        i = rr["ostg"] % 2
        rr["ostg"] += 1
        rr["osl"] = i
        return ostg[:, i, :], b_ostg[i]

    csem = DSem(newsem("d_const"))
    op(POOL, lambda: nc.gpsimd.iota(io[:], [[1, 128]], base=0, channel_multiplier=-1,
                                    allow_small_or_imprecise_dtypes=True), writes=[bf("io")])
    op(DVE, lambda: nc.vector.tensor_single_scalar(ident_f[:], io[:], 0.0, op=ALU.is_equal),
       reads=[bf("io")], writes=[b_const])
    op(DVE, lambda: nc.vector.tensor_single_scalar(masks[:, 0, :], io[:], 0.0, op=ALU.is_lt),
       reads=[bf("io")], writes=[b_const])
    op(DVE, lambda: nc.vector.tensor_single_scalar(masks[:, 1, :], io[:], 0.0, op=ALU.is_ge),
       reads=[bf("io")], writes=[b_const])
    op(DVE, lambda: nc.vector.memset(ones_b[:], 1.0), writes=[b_const])
    op(DVE, lambda: nc.vector.memset(blk1[:], 0.0), writes=[b_const])
    op(DVE, lambda: nc.vector.memset(onesN[:], 1.0 / CCH), writes=[b_const])
    op(DVE, lambda: nc.vector.memset(eps_c[:], EPS), writes=[b_const])
    op(DVE, lambda: nc.vector.tensor_copy(ident_b[:], ident_f[:]), reads=[b_const], writes=[b_const])
    op(DVE, lambda: nc.vector.memset(blk1[0:64, 0:64], 1.0 / HD), reads=[b_const], writes=[b_const])
    op(DVE, lambda: nc.vector.memset(blk1[64:128, 64:128], 1.0 / HD), reads=[b_const], writes=[b_const])

    b_rows = bf("rows")
    row_srcs = [(g_mix, 16), (g_mlp, 16), (b_dw, 8), (g_ln, 8), (b_ln, 8)]
    r0 = 0
    COL = {}
    for nm, (apv, n) in zip(["gmix", "gmlp", "bdw", "gln", "bln"], row_srcs):
        dma(SP, csem, vrows[r0:r0 + n, :], apv.rearrange("(r c) -> r c", c=128), writes=[b_rows])
        COL[nm] = r0
        r0 += n
    COL["qn"] = r0
    for h in range(2):
        dma(SP, csem, vrows[r0:r0 + 1, h * 64:(h + 1) * 64], q_norm.rearrange("(r c) -> r c", c=64),
            writes=[b_rows])
    r0 += 1
    COL["kn"] = r0
    for h in range(2):
        dma(SP, csem, vrows[r0:r0 + 1, h * 64:(h + 1) * 64], k_norm.rearrange("(r c) -> r c", c=64),
            writes=[b_rows])
    r0 += 1
    NR = r0
    wdw_rows = w_dw.rearrange("j (c x) -> (j c) x", x=128)
    dma(SP, csem, wrows[:, 0, :], wdw_rows[0:128, :], writes=[b_rows])
    dma(SP, csem, wrows[0:120, 1, :], wdw_rows[128:248, :], writes=[b_rows])
    dma(SP, csem, es16[:], sinks.partition_broadcast(128), writes=[b_rows])
    b_rows.w = {csem.sem: csem.cnt}

    pe_group(PE, [lambda: nc.tensor.transpose(pf[:, 2, 0:NR], vrows[0:NR, :], ident_f[0:NR, 0:NR])],
             reads=[b_rows, b_const], writes=[bank[2]])
    op(DVE, lambda: nc.vector.tensor_copy(cols[:, 0:NR], pf[:, 2, 0:NR]), reads=[bank[2]], writes=[b_const])
    pe_group(PE, [lambda: nc.tensor.transpose(pf[:, 3, 0:128], wrows[:, 0, :], ident_f[:, :]),
                  lambda: nc.tensor.transpose(pf[:, 3, 128:248], wrows[0:120, 1, :], ident_f[0:120, 0:120])],
             reads=[b_rows, b_const], writes=[bank[3]])
    op(DVE, lambda: nc.vector.tensor_copy(wdwc[:, :], pf[:, 3, 0:248]), reads=[bank[3]], writes=[b_const])
    op(ACT, lambda: nc.scalar.activation(out=es16[:], in_=es16[:], func=AF.Exp), reads=[b_rows], writes=[b_rows])
    for pr in range(2):
        for hf in range(2):
            for g in range(4):
                hh = (2 * pr + hf) * 4 + g
                op(DVE, lambda pr=pr, hf=hf, g=g, hh=hh: nc.vector.tensor_copy(
                    ES[hf * 64:(hf + 1) * 64, pr, g, :],
                    es16[hf * 64:(hf + 1) * 64, hh:hh + 1].to_broadcast([64, 128])),
                   reads=[b_rows], writes=[b_const])

    op(POOL, lambda: nc.gpsimd.iota(io[0:16, 0:64], [[1, 64]], base=0, channel_multiplier=-4,
                                    allow_small_or_imprecise_dtypes=True), reads=[b_const], writes=[bf("io")])
    op(DVE, lambda: nc.vector.tensor_single_scalar(wrows[0:16, 0, 0:64], io[0:16, 0:64], 0.0, op=ALU.is_ge),
       reads=[bf("io"), bank[3]], writes=[b_rows])
    op(DVE, lambda: nc.vector.tensor_single_scalar(wrows[0:16, 1, 0:64], io[0:16, 0:64], 3.0, op=ALU.is_le),
       reads=[bf("io"), bank[3]], writes=[b_rows])
    op(DVE, lambda: nc.vector.tensor_tensor(out=Eblk[:, :], in0=wrows[0:16, 0, 0:64], in1=wrows[0:16, 1, 0:64],
                                            op=ALU.mult), reads=[b_rows], writes=[b_const])
    pe_group(PE, [lambda: nc.tensor.matmul(pf[0:64, 2, 0:64], lhsT=Eblk[:, :], rhs=Eblk[:, :], start=True,
                                           stop=True)], reads=[b_const], writes=[bank[2]])
    op(DVE, lambda: nc.vector.tensor_tensor(out=maskN[:, :], in0=pf[0:64, 2, 0:64], in1=masks[0:64, 1, 0:64],
                                            op=ALU.mult), reads=[bank[2], b_const], writes=[b_const])

    op(POOL, lambda: nc.gpsimd.memset(maskCA[:, :, :], 0.0), writes=[b_const])
    for b in range(NSMP):
        op(POOL, lambda b=b: nc.gpsimd.tensor_copy(maskCA[:, b, b * SD:(b + 1) * SD], masks[:, 0, 0:SD]),
           reads=[b_const], writes=[b_const])

    def col(nm, i=0):
        c = COL[nm] + i
        return cols[:, c:c + 1]

    psem = [DSem(newsem(f"d_p{i}")) for i in range(NPANEL)]

    def std_panel(idx, dst, wb, wsrc, r_lo, c_lo):
        src = wsrc[r_lo:r_lo + 2048, c_lo:c_lo + 512].rearrange("(kc p) c -> p kc c", p=128)
        dma(POOL, psem[idx], dst.rearrange("p (kc c) -> p kc c", c=512), src, writes=[wb])

    def q_panel(idx, dst, wb, pr):
        stg = PH[:, :].rearrange("p (kc c) -> p kc c", c=512)
        dma(POOL, psem[idx], stg, w_in[:, 512 * pr:512 * pr + 512].rearrange("(kc p) c -> p kc c", p=128),
            writes=b_PH)
        dst3 = dst.rearrange("p (kc c) -> p kc c", c=512)
        for g in range(4):
            for hf in range(2):
                so = (hf * 4 + g) * 64
                do = g * 128 + hf * 64
                op(POOL, lambda so=so, do=do: nc.gpsimd.tensor_copy(dst3[:, :, do:do + 64], stg[:, :, so:so + 64]),
                   reads=b_PH, writes=[wb])

    def wout_panel(idx, dst, wb, n):
        dst3 = dst.rearrange("p (kc c) -> p kc c", c=512)
        for pr in range(2):
            for hf in range(2):
                rr0 = (2 * pr + hf) * 256
                src = w_out[rr0:rr0 + 256, n * 512:(n + 1) * 512].rearrange("(g d) c -> d g c", g=4)
                dma(POOL, psem[idx], dst3[hf * 64:(hf + 1) * 64, pr * 4:pr * 4 + 4, :], src, writes=[wb])
        src = w_out[1024:2048, n * 512:(n + 1) * 512].rearrange("(kc p) c -> p kc c", p=128)
        dma(POOL, psem[idx], dst3[:, 8:16, :], src, writes=[wb])

    def ag_panel(idx, dst, wb, i):
        dst3 = dst.rearrange("p (kc c) -> p kc c", c=512)
        for s2 in range(2):
            c = 2 * i + s2
            for t2, base in enumerate((1536, 2560)):
                f0 = base + c * 128
                src = w_in[:, f0:f0 + 128].rearrange("(kc p) c -> p kc c", p=128)
                o = (2 * s2 + t2) * 128
                dma(POOL, psem[idx], dst3[:, :, o:o + 128], src, writes=[wb])

    pp_emit = {}
    for i in range(4):
        pp_emit[P_AG[i]] = (lambda dst, wb, i=i: ag_panel(P_AG[i], dst, wb, i))
    pp_emit[P_Q] = lambda dst, wb: q_panel(P_Q, dst, wb, 0)
    pp_emit[P_Q + 1] = lambda dst, wb: q_panel(P_Q + 1, dst, wb, 1)
    pp_emit[P_KV] = lambda dst, wb: std_panel(P_KV, dst, wb, w_in, 0, 1024)
    for n in range(4):
        pp_emit[P_WO + n] = (lambda dst, wb, n=n: wout_panel(P_WO + n, dst, wb, n))
    for qt in range(4):
        for i in range(4):
            pp_emit[P_FF + qt * 8 + i] = (lambda dst, wb, qt=qt, i=i: std_panel(P_FF + qt * 8 + i, dst, wb, w_up, 0,
                                                                                (qt * 4 + i) * 512))
        for n in range(4):
            pp_emit[P_FF + qt * 8 + 4 + n] = (lambda dst, wb, qt=qt, n=n: std_panel(P_FF + qt * 8 + 4 + n, dst, wb,
                                                                                    w_down, qt * 2048, n * 512))

    wsem = [DSem(newsem(f"d_w{i}")) for i in range(NSLOT)]
    wstate = {"next": 0, "total": 0}

    def wload_upto(gidx):
        while wstate["next"] <= min(gidx, wstate["total"] - 1):
            g = wstate["next"]
            slot = g % NSLOT
            idx = g % NPANEL
            if g < NPANEL and idx not in P_DG:
                pp_emit[idx](W[:, slot, :], b_W[slot])
                b_W[slot].w[psem[idx].sem] = psem[idx].cnt
                dma(SP, psem[idx], wscr[idx], W[:, slot, :], reads=[b_W[slot]], writes=[b_panel[idx]])
            else:
                dma(SP, wsem[slot], W[:, slot, :], wscr[idx], reads=[b_panel[idx]], writes=[b_W[slot]])
            wstate["next"] += 1

    def panel(gidx):
        wload_upto(gidx + NSLOT - 1)
        slot = gidx % NSLOT
        return W[:, slot, :].rearrange("p (kc c) -> p kc c", c=512), b_W[slot]

    def panel_flat(gidx):
        wload_upto(gidx + NSLOT - 1)
        slot = gidx % NSLOT
        return W[:, slot, :], b_W[slot]

    def gen_diag():
        for i in range(4):
            if i % 2 == 0:
                stg, sbufs = MH[:, :, :].rearrange("p a b -> p (a b)"), b_MHb
            else:
                stg, sbufs = PH[:, :], b_PH
            for s2 in range(2):
                c = 2 * i + s2
                for j in range(CW):
                    r = s2 * CW + j
                    wc = wdwc[:, j * 8 + c:j * 8 + c + 1]
                    edge = sbufs if (r < 2 or r >= 2 * CW - 2) else []
                    if r % 2 == 0:
                        op(ACT, lambda r=r, wc=wc, stg=stg: nc.scalar.activation(
                            out=stg[:, r * 128:(r + 1) * 128], in_=ident_b[:, :], func=AF.Identity, scale=wc),
                           reads=[b_const], writes=edge)
                    else:
                        op(DVE, lambda r=r, wc=wc, stg=stg: nc.vector.tensor_scalar(
                            out=stg[:, r * 128:(r + 1) * 128], in0=ident_b[:, :], scalar1=wc, scalar2=None,
                            op0=ALU.mult), reads=[b_const], writes=edge)
            op(DVE, lambda stg=stg: nc.vector.memset(stg[:, 2 * CW * 128:PW], 0.0), writes=sbufs)
            dma(SP, psem[P_DG[i]], wscr[P_DG[i]], stg, reads=sbufs, writes=[b_panel[P_DG[i]]])
            b_panel[P_DG[i]].w[psem[P_DG[i]].sem] = psem[P_DG[i]].cnt

    xsem = DSem(newsem("d_xin"))
    xrsem = [DSem(newsem(f"d_xr{i}")) for i in range(4)]
    ysem = [osem(f"d_y{i}") for i in range(4)]
    osm = [osem("d_o0"), osem("d_o1")]
    osm_misc = osem("d_om")
    b_xin = bf("xin")
    b_xhat = bf("xhat")
    b_XT = bf("XT")
    b_MHa = [bf(f"MHa{i}") for i in range(4)]
    b_MHcb = [bf(f"MHc{i}") for i in range(4)]
    b_MHb = b_MHa + b_MHcb
    b_qT = bf("qT")
    b_kT = bf("kT")
    b_vB = bf("vB")
    b_PH = [bf("PH0"), bf("PH1")]
    b_uh = bf("uhalo")
    b_kf = bf("kf")

    def rstd_from_ssq(ssq_ap, ssq_buf, n_feat, npart):
        st, sbuf_ = get_stat()
        op(ACT, lambda: nc.scalar.activation(out=st[:npart, 1:2], in_=ssq_ap, func=AF.Sqrt,
                                             scale=1.0 / n_feat, bias=eps_c[:npart, :]),
           reads=[ssq_buf, b_const], writes=[sbuf_])
        op(DVE, lambda: nc.vector.reciprocal(st[:npart, 2:3], st[:npart, 1:2]), reads=[sbuf_], writes=[sbuf_])
        return st[:npart, 2:3], sbuf_

    def nt_front(src_ap, src_buf, npart):
        st, sbuf_ = get_stat()
        op(ACT, lambda: nc.scalar.activation(out=xhat[:npart, :], in_=src_ap, func=AF.Square,
                                             accum_out=st[:npart, 0:1]),
           reads=[src_buf], writes=[b_xhat, sbuf_])
        rs, rsb = rstd_from_ssq(st[:npart, 0:1], sbuf_, D, npart)
        op(DVE, lambda: nc.vector.tensor_scalar(out=xhat[:npart, :], in0=src_ap, scalar1=rs, scalar2=None,
                                                op0=ALU.mult),
           reads=[src_buf, rsb], writes=[b_xhat])

    def nt_back(npart, dstT, b_dst, tok_off, gname):
        pe_group(PE, [(lambda c=c: nc.tensor.transpose(trp[:, c, 0:npart], xhat[:npart, c * 128:(c + 1) * 128],
                                                        ident_b[:npart, :npart])) for c in range(16)],
                 reads=[b_xhat, b_const], writes=[b_tr])
        g0 = COL[gname]
        op(DVE, lambda: nc.vector.tensor_tensor(
            out=dstT[:, :, tok_off:tok_off + npart], in0=trp[:, :, 0:npart],
            in1=cols[:, g0:g0 + 16].unsqueeze(2).to_broadcast([128, 16, npart]), op=ALU.mult),
           reads=[b_tr, b_const], writes=(b_dst if isinstance(b_dst, list) else [b_dst]))

    def norm_transpose(src_ap, src_buf, npart, dstT, b_dst, tok_off, gname):
        nt_front(src_ap, src_buf, npart)
        nt_back(npart, dstT, b_dst, tok_off, gname)

    def prompt_phase_a_steps(seq, j):
        row0 = seq * S + j * T

        def front(m):
            dma(SP, xsem, xin[:, :], xp[row0 + m * 128:row0 + (m + 1) * 128, :], writes=[b_xin])
            nt_front(xin[:, :], b_xin, 128)

        def back(m):
            nt_back(128, XT, b_XT, m * 128, "gmix")

        steps = [lambda: front(0)]
        for m in range(1, 4):
            steps.append(lambda m=m: (back(m - 1), front(m)))
        steps.append(lambda: back(3))
        return steps

    def prompt_phase_a(seq, j):
        for st_ in prompt_phase_a_steps(seq, j):
            st_()

    def prompt_tile(ti, seq, j, gbase, next_a=None):
        row0 = seq * S + j * T
        first = (j == 0)
        last = (j == S // T - 1)
        NBK = 4

        if first:
            op(POOL, lambda: nc.gpsimd.memset(uTb[:, :, 0:CB], 0.0), writes=b_u)
            op(POOL, lambda: nc.gpsimd.memset(vB[:, 0, :], 0.0), writes=[b_vB])
            op(POOL, lambda: nc.gpsimd.memset(kT[:, :, 0:128], 0.0), writes=[b_kT])
        else:
            op(POOL, lambda: nc.gpsimd.tensor_copy(uTb[:, :, 0:CB], uTb[:, :, T:T + CB]), reads=b_u, writes=b_u)
            op(POOL, lambda: nc.gpsimd.tensor_copy(vB[:, 0, :], vB[:, 4, :]), reads=[b_vB], writes=[b_vB])
            op(POOL, lambda: nc.gpsimd.tensor_copy(kT[:, :, 0:128], kT[:, :, T:T + 128]), reads=[b_kT],
               writes=[b_kT])

        for m in range(NBK):
            dma(SP, xrsem[m], x1[:, m, :], xp[row0 + m * 128:row0 + (m + 1) * 128, :], writes=[b_x1[m]])

        def wgroup(Wp, bW, sub, ps, pbuf, extra_reads=()):
            pe_group(PE, [(lambda kc=kc: nc.tensor.matmul(ps, lhsT=Wp[:, kc, sub * 128:(sub + 1) * 128],
                                                          rhs=XT[:, kc, :], start=(kc == 0), stop=(kc == 15)))
                          for kc in range(16)],
                     reads=[bW, b_XT] + list(extra_reads), writes=[pbuf])

        def ag_groups(i):
            Wp, bW = panel(gbase + P_AG[i])
            for s2 in range(2):
                c = 2 * i + s2
                psA, pbA = get_mm4()
                wgroup(Wp, bW, 2 * s2, psA, pbA)
                psB, pbB = get_mm4()
                wgroup(Wp, bW, 2 * s2 + 1, psB, pbB)
                sg, sgb = get_t32()
                op(ACT, lambda psB=psB, sg=sg: nc.scalar.activation(out=sg, in_=psB, func=AF.Sigmoid), reads=[pbB],
                   writes=[sgb])
                op(DVE, lambda c=c, psA=psA, sg=sg: nc.vector.tensor_tensor(out=uTb[:, c, CB:CB + T], in0=psA,
                                                                           in1=sg, op=ALU.mult),
                   reads=[pbA, sgb], writes=[b_u[c]])
                if last:
                    op(DVE, lambda c=c, psA=psA, sg=sg: nc.vector.tensor_tensor(
                        out=uhalo[:, c, :], in0=psA[:, T - CB:T], in1=sg[:, T - CB:T], op=ALU.mult),
                       reads=[pbA, sgb], writes=[b_uh])

        pend = []

        def ln_stat(c, yq, yqb):
            pe_group(PE, [lambda: nc.tensor.matmul(pf[:, 4, :], lhsT=onesN[:, :], rhs=ybf[:, c, :],
                                                   start=(c == 0), stop=(c == 7)),
                          lambda: nc.tensor.matmul(pf[:, 5, :], lhsT=onesN[:, :], rhs=yq,
                                                   start=(c == 0), stop=(c == 7))],
                     reads=[b_yc[c], yqb, b_const], writes=[bank[4], bank[5]])

        def conv_pe(i):
            Dp, bD = panel_flat(gbase + P_DG[i])
            for s2 in range(2):
                c = 2 * i + s2
                ps, pb = pf[:, c % 4, :], bank[c % 4]
                pe_group(PE, [(lambda j=j: nc.tensor.matmul(
                    ps, lhsT=Dp[:, (s2 * CW + j) * 128:(s2 * CW + j + 1) * 128], rhs=uTb[:, c, j:j + T],
                    start=(j == 0), stop=(j == CW - 1))) for j in range(CW)],
                    reads=[bD, b_u[c]], writes=[pb])
                if pend:
                    ln_stat(*pend.pop())
                yq, yqb = get_t16()
                op(ACT, lambda c=c, ps=ps: nc.scalar.activation(out=ybf[:, c, :], in_=ps, func=AF.Identity,
                                                               bias=col("bdw", c)), reads=[pb, b_const],
                   writes=[b_yc[c]])
                op(ACT, lambda c=c, ps=ps, yq=yq: nc.scalar.activation(out=yq, in_=ps, func=AF.Square,
                                                                      bias=col("bdw", c)), reads=[pb, b_const],
                   writes=[yqb])
                pend.append((c, yq, yqb))
            if i == 3:
                ln_stat(*pend.pop())

        def qk_norm(ps, pb, dst_ap, dst_buf, wcol, kf_ap=None):
            zs, zb = get_t32()
            sq, qb = get_t16()
            op(ACT, lambda: nc.scalar.activation(out=sq, in_=ps, func=AF.Square), reads=[pb], writes=[qb])
            op(ACT, lambda: nc.scalar.copy(zs, ps), reads=[pb], writes=[zb])
            ss, ssb = pf[:, 2 + (rr["mm"] % 2), :], bank[2 + (rr["mm"] % 2)]
            pe_group(PE, [lambda: nc.tensor.matmul(ss, lhsT=blk1[:, :], rhs=sq, start=True, stop=True)],
                     reads=[qb, b_const], writes=[ssb])
            sd_, sdb = get_t32()
            op(ACT, lambda: nc.scalar.activation(out=sd_, in_=ss, func=AF.Sqrt, bias=eps_c[:, :]),
               reads=[ssb, b_const], writes=[sdb])
            op(DVE, lambda: nc.vector.reciprocal(sd_, sd_), reads=[sdb], writes=[sdb])
            op(DVE, lambda: nc.vector.scalar_tensor_tensor(out=dst_ap, in0=zs, scalar=wcol, in1=sd_,
                                                           op0=ALU.mult, op1=ALU.mult),
               reads=[zb, sdb, b_const], writes=[dst_buf])
            if kf_ap is not None:
                op(DVE, lambda: nc.vector.scalar_tensor_tensor(out=kf_ap, in0=zs[:, T - 128:T], scalar=wcol,
                                                               in1=sd_[:, T - 128:T], op0=ALU.mult, op1=ALU.mult),
                   reads=[zb, sdb, b_const], writes=[b_kf])

        for pi in range(2):
            Wp, bW = panel(gbase + P_Q + pi)
            for sub in range(4):
                c = pi * 4 + sub
                ps, pb = get_mm()
                wgroup(Wp, bW, sub, ps, pb)
                qk_norm(ps, pb, qT[:, c, :], b_qT, col("qn"))
        Wp, bW = panel(gbase + P_KV)
        for sub in range(2):
            ps, pb = get_mm()
            wgroup(Wp, bW, sub, ps, pb)
            qk_norm(ps, pb, kT[:, sub, 128:128 + T], b_kT, col("kn"), kf_ap=(kf[:, sub, :] if last else None))
        for m in range(NBK):
            ps, pb = get_mm()
            pe_group(PE, [(lambda kc=kc: nc.tensor.matmul(ps[:, 0:KVW], lhsT=XT[:, kc, m * 128:(m + 1) * 128],
                                                          rhs=Wp[:, kc, 256:512], start=(kc == 0), stop=(kc == 15)))
                          for kc in range(16)], reads=[bW, b_XT], writes=[pb])
            op(ACT, lambda m=m, ps=ps: nc.scalar.copy(vB[:, 1 + m, :], ps[:, 0:KVW]), reads=[pb], writes=[b_vB])
            if last and m == NBK - 1:
                og, ob = get_ostg()
                op(ACT, lambda ps=ps, og=og: nc.scalar.copy(og[:, 0:KVW], ps[:, 0:KVW]), reads=[pb], writes=[ob])
                dma(SP, osm[rr['osl']], nvp[seq], og[:, 0:KVW], reads=[ob])
        if last:
            pe_group(PE, [(lambda s2=s2: nc.tensor.transpose(pf[:, 3, s2 * 128:(s2 + 1) * 128], kf[:, s2, :],
                                                              ident_f[:, :])) for s2 in range(2)],
                     reads=[b_kf, b_const], writes=[bank[3]])
            og, ob = get_ostg()
            op(ACT, lambda og=og: nc.scalar.copy(og[:, 0:KVW], pf[:, 3, 0:KVW]), reads=[bank[3]], writes=[ob])
            dma(SP, osm[rr['osl']], nkp[seq], og[:, 0:KVW], reads=[ob])

        def qk_block(m):
            slot = m % 2
            pT = PH[:, slot * 4096:(slot + 1) * 4096].rearrange("p (ch kv n) -> p ch kv n", ch=2, kv=4)
            chunks = [1] if (first and m == 0) else [0, 1]
            for kvh in range(4):
                hf, pr = kvh % 2, kvh // 2
                for ch in chunks:
                    kb = m + ch
                    sidx = rr["mm4"] % 4
                    rr["mm4"] += 1
                    ps, pb = pf[:, sidx, :], bank[sidx]
                    pe_group(PE, [lambda: nc.tensor.matmul(
                        ps.rearrange("p (g q) -> p g q", g=4),
                        lhsT=kT[hf * 64:(hf + 1) * 64, pr, kb * 128:(kb + 1) * 128],
                        rhs=qT[hf * 64:(hf + 1) * 64, pr * 4:pr * 4 + 4, m * 128:(m + 1) * 128],
                        start=True, stop=True)], reads=[b_kT, b_qT], writes=[pb])
                    op(ACT, lambda ch=ch, kvh=kvh, ps=ps: nc.scalar.activation(
                        out=pT[:, ch, kvh, :], in_=ps, func=AF.Exp, scale=HD ** -0.5),
                       reads=[pb], writes=[b_PH[slot]])
            for ch in chunks:
                op(DVE, lambda ch=ch: nc.vector.tensor_tensor(
                    out=pT[:, ch, :, :].rearrange("p kv (g q) -> p (kv g) q", g=4),
                    in0=pT[:, ch, :, :].rearrange("p kv (g q) -> p (kv g) q", g=4),
                    in1=masks[:, ch, :].unsqueeze(1).to_broadcast([128, 16, 128]), op=ALU.mult),
                   reads=[b_PH[slot], b_const], writes=[b_PH[slot]])
            return chunks

        def pv_block(m, chunks):
            slot = m % 2
            pT = PH[:, slot * 4096:(slot + 1) * 4096].rearrange("p (ch kv n) -> p ch kv n", ch=2, kv=4)
            for pr in range(2):
                if pr == 0:
                    Ob, Db, obuf = pf[:, 4, :], pf[:, 5, :], [bank[4], bank[5]]
                else:
                    Ob, Db, obuf = trf[:, 0, :], trf[:, 1, :], [b_tr]
                fns = []
                for hf in range(2):
                    kvh = 2 * pr + hf
                    for i, ch in enumerate(chunks):
                        kb = m + ch
                        fns.append(lambda hf=hf, kvh=kvh, ch=ch, kb=kb, i=i, Ob=Ob: nc.tensor.matmul(
                            Ob[hf * 64:(hf + 1) * 64, :], lhsT=vB[:, kb, kvh * 64:(kvh + 1) * 64],
                            rhs=pT[:, ch, kvh, :], start=(i == 0), stop=(i == len(chunks) - 1)))
                        fns.append(lambda hf=hf, kvh=kvh, ch=ch, i=i, Db=Db: nc.tensor.matmul(
                            Db[hf * 64:(hf + 1) * 64, :], lhsT=ones_b[:, :],
                            rhs=pT[:, ch, kvh, :], start=(i == 0), stop=(i == len(chunks) - 1)))
                pe_group(PE, fns, reads=[b_vB, b_PH[slot], b_const], writes=obuf)
                dn, dnb = get_t32()
                op(DVE, lambda pr=pr, dn=dn, Db=Db: nc.vector.tensor_tensor(
                    out=dn, in0=Db, in1=ES[:, pr, :, :].rearrange("p g q -> p (g q)"), op=ALU.add),
                   reads=obuf + [b_const], writes=[dnb])
                op(DVE, lambda dn=dn: nc.vector.reciprocal(dn, dn), reads=[dnb], writes=[dnb])
                op(DVE, lambda pr=pr, dn=dn, Ob=Ob: nc.vector.tensor_tensor(
                    out=MH[:, pr * 4:pr * 4 + 4, m * 128:(m + 1) * 128],
                    in0=Ob.rearrange("p (g q) -> p g q", g=4),
                    in1=dn.rearrange("p (g q) -> p g q", g=4), op=ALU.mult),
                   reads=obuf + [dnb], writes=[b_MHa[m]])

        if ti == 0:
            gen_diag()

        prev = None
        for m in range(NBK):
            chs = qk_block(m)
            if prev is not None:
                pv_block(*prev)
            ag_groups(m)
            prev = (m, chs)
        pv_block(*prev)

        conv_pe(0)
        conv_pe(1)
        conv_pe(2)
        conv_pe(3)
        mean, mb_ = t32[:, NT32 - 2, :], b_t32[NT32 - 2]
        var, vb_ = t32[:, NT32 - 1, :], b_t32[NT32 - 1]
        op(ACT, lambda: nc.scalar.copy(mean, pf[:, 4, :]), reads=[bank[4]], writes=[mb_])
        op(ACT, lambda: nc.scalar.copy(var, pf[:, 5, :]), reads=[bank[5]], writes=[vb_])
        msq, msb = get_t32()
        op(DVE, lambda: nc.vector.tensor_tensor(out=msq, in0=mean, in1=mean, op=ALU.mult), reads=[mb_],
           writes=[msb])
        op(DVE, lambda: nc.vector.tensor_tensor(out=var, in0=var, in1=msq, op=ALU.subtract),
           reads=[msb, vb_], writes=[vb_])
        op(ACT, lambda: nc.scalar.activation(out=var, in_=var, func=AF.Sqrt, bias=eps_c[:, :]),
           reads=[vb_, b_const], writes=[vb_])
        op(DVE, lambda: nc.vector.reciprocal(var, var), reads=[vb_], writes=[vb_])

        def ln_chunk(c):
            t1, t1b = get_t32()
            op(DVE, lambda: nc.vector.tensor_tensor(out=t1, in0=ybf[:, c, :], in1=mean, op=ALU.subtract),
               reads=[b_yc[c], mb_], writes=[t1b])
            op(DVE, lambda: nc.vector.tensor_tensor(out=t1, in0=t1, in1=var, op=ALU.mult),
               reads=[t1b, vb_], writes=[t1b])
            op(ACT, lambda: nc.scalar.activation(out=MH[:, 8 + c, :], in_=t1, func=AF.Silu,
                                                 bias=col("bln", c), scale=col("gln", c)),
               reads=[t1b, b_const], writes=b_MHcb)
        for c in range(8):
            ln_chunk(c)
        if last:
            pe_group(PE, [(lambda c=c: nc.tensor.transpose(pf[0:CB, 2 + c // 4, (c % 4) * 128:(c % 4 + 1) * 128],
                                                            uhalo[:, c, :], ident_f[:, :])) for c in range(8)],
                     reads=[b_uh, b_const], writes=[bank[2], bank[3]])
            for hh in range(2):
                og, ob = get_ostg()
                op(ACT, lambda og=og, hh=hh: nc.scalar.copy(og[0:CB, :], pf[0:CB, 2 + hh, :]), reads=[bank[2 + hh]],
                   writes=[ob])
                dma(SP, osm[rr['osl']], ncp[seq][:, hh * 512:(hh + 1) * 512], og[0:CB, :], reads=[ob])


        for n in range(4):
            Wp, bW = panel(gbase + P_WO + n)
            if n == 0:
                for m in range(NBK):
                    pe_group(PE, [(lambda kc=kc: nc.tensor.matmul(pf[:, m, :], lhsT=MH[:, kc, m * 128:(m + 1) * 128],
                                                                  rhs=Wp[:, kc, :], start=(kc == 0), stop=False))
                                  for kc in range(8)], reads=[bW, b_MHa[m]], writes=[bank[m]])
                for m in range(NBK):
                    pe_group(PE, [(lambda kc=kc: nc.tensor.matmul(pf[:, m, :], lhsT=MH[:, kc, m * 128:(m + 1) * 128],
                                                                  rhs=Wp[:, kc, :], start=False, stop=(kc == 15)))
                                  for kc in range(8, 16)], reads=[bW, b_MHcb[m]], writes=[bank[m]])
                    op(DVE, lambda m=m: nc.vector.tensor_tensor(
                        out=x1[:, m, 0:512], in0=pf[:, m, :], in1=x1[:, m, 0:512], op=ALU.add),
                       reads=[bank[m]], writes=[b_x1[m]])
                continue
            for m in range(NBK):
                ps, pb = get_mm()
                pe_group(PE, [(lambda kc=kc: nc.tensor.matmul(ps, lhsT=MH[:, kc, m * 128:(m + 1) * 128],
                                                              rhs=Wp[:, kc, :], start=(kc == 0), stop=(kc == 15)))
                              for kc in range(16)], reads=[bW, b_MHa[m], b_MHcb[m]], writes=[pb])
                op(DVE, lambda m=m, n=n, ps=ps: nc.vector.tensor_tensor(
                    out=x1[:, m, n * 512:(n + 1) * 512], in0=ps, in1=x1[:, m, n * 512:(n + 1) * 512], op=ALU.add),
                   reads=[pb], writes=[b_x1[m]])
                if n == 3:
                    if m >= 1:
                        nt_back(128, MH, [b_MHa[m - 1], b_MHcb[m - 1]], (m - 1) * 128, "gmlp")
                    nt_front(x1[:, m, :], b_x1[m], 128)
        nt_back(128, MH, [b_MHa[NBK - 1], b_MHcb[NBK - 1]], (NBK - 1) * 128, "gmlp")

        hff = PH[:, :].rearrange("p (f t) -> p f t", t=T)
        for qt in range(4):
            for i in range(4):
                if qt >= 2 and next_a:
                    next_a.pop(0)()
                Wp, bW = panel(gbase + P_FF + qt * 8 + i)
                for sub in range(4):
                    ps, pb = get_mm()
                    pe_group(PE, [(lambda kc=kc: nc.tensor.matmul(ps, lhsT=Wp[:, kc, sub * 128:(sub + 1) * 128],
                                                                  rhs=MH[:, kc, :], start=(kc == 0),
                                                                  stop=(kc == 15))) for kc in range(16)],
                             reads=[bW] + b_MHb, writes=[pb])
                    r_, rb = get_t32()
                    op(ACT, lambda ps=ps, r_=r_: nc.scalar.activation(out=r_, in_=ps, func=AF.Relu), reads=[pb],
                       writes=[rb])
                    f = i * 4 + sub
                    op(ACT, lambda r_=r_, f=f: nc.scalar.activation(out=hff[:, f, :], in_=r_, func=AF.Square),
                       reads=[rb], writes=b_PH)
            for n in range(4):
                Wp, bW = panel(gbase + P_FF + qt * 8 + 4 + n)
                for m in range(NBK):
                    ps, pb = pf[:, 2 + m, :], bank[2 + m]
                    pe_group(PE, [(lambda kc=kc: nc.tensor.matmul(ps, lhsT=hff[:, kc, m * 128:(m + 1) * 128],
                                                                  rhs=Wp[:, kc, :], start=(kc == 0),
                                                                  stop=(kc == 15))) for kc in range(16)],
                             reads=[bW] + b_PH, writes=[pb])
                    op(DVE, lambda m=m, n=n, ps=ps: nc.vector.tensor_tensor(
                        out=x1[:, m, n * 512:(n + 1) * 512], in0=ps, in1=x1[:, m, n * 512:(n + 1) * 512],
                        op=ALU.add), reads=[pb], writes=[b_x1[m]])
        for m in range(NBK):
            dma(SP, ysem[m], yp[row0 + m * 128:row0 + (m + 1) * 128, :], x1[:, m, :], reads=[b_x1[m]])

    def sample_tile(gbase):
        NTK = NS_TOK
        uS = UY[:, :].rearrange("p (c b r) -> p c b r", c=8, r=CB + SD)
        x1b2 = x1[:, 2, :].bitcast(BF16)
        pTc = x1b2[:, 0:1024]
        pTn = x1b2[0:64, 1024:2048].rearrange("p (h a n) -> p h a n", a=2, h=2)
        ckst = x1[:, 3, :].rearrange("p (b c) -> p b c", c=KVW)
        kcT = PH[:, 0:4096].rearrange("p (b a k) -> p b a k", b=NSMP, a=2)
        cvB = PH[:, 4096:8192].rearrange("p (b c) -> p b c", c=KVW)
        b_scr2, b_scr3 = b_x1[2], b_x1[3]

        dma(SP, osm_misc, nks[:, 0:128 - SD, :], ck[:, SD:128, :])
        dma(SP, osm_misc, nvs[:, 0:128 - SD, :], cv[:, SD:128, :])
        dma(SP, osm_misc, ncs[:, 0:CB - SD, :], sc[:, SD:CB, :])

        dma(SP, xsem, xin[0:NTK, :], xs[:, :], writes=[b_xin])
        norm_transpose(xin[0:NTK, :], b_xin, NTK, XT, b_XT, 0, "gmix")
        dma(SP, xrsem[0], x1[0:NTK, 0, :], xs[:, :], writes=[b_x1[0]])

        for g4 in range(4):
            dma(SP, xsem, xin[0:120, 0:CCH], sc[4 * g4:4 * g4 + 4].rearrange("b r c -> (b r) c"), writes=[b_xin])
            pe_group(PE, [(lambda c=c: nc.tensor.transpose(pf[:, 2 + c // 4, (c % 4) * 120:(c % 4 + 1) * 120],
                                                            xin[0:120, c * 128:(c + 1) * 128],
                                                            ident_f[0:120, 0:120])) for c in range(8)],
                     reads=[b_xin, b_const], writes=[bank[2], bank[3]])
            for h in range(2):
                op(ACT, lambda h=h, g4=g4: nc.scalar.copy(
                    uS[:, 4 * h:4 * h + 4, 4 * g4:4 * g4 + 4, 0:CB],
                    pf[:, 2 + h, 0:480].rearrange("p (c b r) -> p c b r", c=4, b=4)),
                   reads=[bank[2 + h]], writes=b_u)

        if _DBG['stop'] <= 1:
            return
        for h in range(2):
            dma(SP, xrsem[3], ckst, ck[8 * h:8 * h + 8].rearrange("b k c -> k b c"), writes=[b_scr3])
            for bp in range(4):
                bk = bank[2 + bp % 2]
                fns = []
                for bb in range(2):
                    for pr in range(2):
                        fns.append(lambda bb=bb, pr=pr, bp=bp: nc.tensor.transpose(
                            pf[:, 2 + bp % 2, (bb * 2 + pr) * 128:(bb * 2 + pr + 1) * 128],
                            ckst[:, bp * 2 + bb, pr * 128:(pr + 1) * 128], ident_f[:, :]))
                pe_group(PE, fns, reads=[b_scr3, b_const], writes=[bk])
                b0 = 8 * h + bp * 2
                op(ACT, lambda bp=bp, b0=b0: nc.scalar.copy(
                    kcT[:, b0:b0 + 2, :, :], pf[:, 2 + bp % 2, :].rearrange("p (b a k) -> p b a k", b=2, a=2)),
                   reads=[bk], writes=b_PH)
        for h in range(2):
            dma(SP, xrsem[3], ckst, cv[8 * h:8 * h + 8].rearrange("b k c -> k b c"), writes=[b_scr3])
            op(ACT, lambda h=h: nc.scalar.copy(cvB[:, 8 * h:8 * h + 8, :], ckst), reads=[b_scr3], writes=b_PH)

        if _DBG['stop'] <= 2:
            return
        def wgroup_s(Wp, bW, sub, ps, pbuf):
            pe_group(PE, [(lambda kc=kc: nc.tensor.matmul(ps[:, 0:NTK], lhsT=Wp[:, kc, sub * 128:(sub + 1) * 128],
                                                          rhs=XT[:, kc, 0:NTK], start=(kc == 0), stop=(kc == 15)))
                          for kc in range(16)], reads=[bW, b_XT], writes=[pbuf])

        def conv_chunk_s(c):
            acc_, ab = get_t32()
            acc = acc_[:, 0:NTK].rearrange("p (b i) -> p b i", i=SD)
            op(DVE, lambda: nc.vector.tensor_scalar(out=acc, in0=uS[:, c, :, 0:SD], scalar1=wdwc[:, c:c + 1],
                                                    scalar2=None, op0=ALU.mult),
               reads=[b_u[c], b_const], writes=[ab])
            for jj in range(1, CW - 1):
                op(DVE, lambda jj=jj: nc.vector.scalar_tensor_tensor(
                    out=acc, in0=uS[:, c, :, jj:jj + SD], scalar=wdwc[:, jj * 8 + c:jj * 8 + c + 1], in1=acc,
                    op0=ALU.mult, op1=ALU.add), reads=[b_u[c], ab], writes=[ab])
            jj = CW - 1
            op(DVE, lambda: nc.vector.scalar_tensor_tensor(
                out=x1[:, 1, c * NTK:(c + 1) * NTK].rearrange("p (b i) -> p b i", i=SD),
                in0=uS[:, c, :, jj:jj + SD], scalar=wdwc[:, jj * 8 + c:jj * 8 + c + 1],
                in1=acc, op0=ALU.mult, op1=ALU.add), reads=[ab, b_u[c]], writes=[b_x1[1]])

        if _DBG['stop'] <= 3:
            return
        def qk_norm_s(ps, pb, dst_ap, dst_buf, wcol, kf_ap=None):
            zs, zb = get_t32()
            sq, qb = get_t16()
            op(ACT, lambda: nc.scalar.activation(out=sq[:, 0:NTK], in_=ps[:, 0:NTK], func=AF.Square), reads=[pb],
               writes=[qb])
            op(ACT, lambda: nc.scalar.copy(zs[:, 0:NTK], ps[:, 0:NTK]), reads=[pb], writes=[zb])
            si = 2 + (rr["mm"] % 2)
            ss, ssb = pf[:, si, 0:NTK], bank[si]
            pe_group(PE, [lambda: nc.tensor.matmul(ss, lhsT=blk1[:, :], rhs=sq[:, 0:NTK], start=True, stop=True)],
                     reads=[qb, b_const], writes=[ssb])
            sd_, sdb = get_t32()
            op(ACT, lambda: nc.scalar.activation(out=sd_[:, 0:NTK], in_=ss, func=AF.Sqrt, bias=eps_c[:, :]),
               reads=[ssb, b_const], writes=[sdb])
            op(DVE, lambda: nc.vector.reciprocal(sd_[:, 0:NTK], sd_[:, 0:NTK]), reads=[sdb], writes=[sdb])
            op(DVE, lambda: nc.vector.scalar_tensor_tensor(out=dst_ap, in0=zs[:, 0:NTK], scalar=wcol,
                                                           in1=sd_[:, 0:NTK], op0=ALU.mult, op1=ALU.mult),
               reads=[zb, sdb, b_const], writes=[dst_buf])
            if kf_ap is not None:
                op(DVE, lambda: nc.vector.scalar_tensor_tensor(out=kf_ap, in0=zs[:, 0:NTK], scalar=wcol,
                                                               in1=sd_[:, 0:NTK], op0=ALU.mult, op1=ALU.mult),
                   reads=[zb, sdb, b_const], writes=[b_kf])

        for pi in range(2):
            Wp, bW = panel(gbase + P_Q + pi)
            for sub in range(4):
                c = pi * 4 + sub
                ps, pb = get_mm()
                wgroup_s(Wp, bW, sub, ps, pb)
                qk_norm_s(ps, pb, qT[:, c, 0:NTK], b_qT, col("qn"))
        Wp, bW = panel(gbase + P_KV)
        for sub in range(2):
            ps, pb = get_mm()
            wgroup_s(Wp, bW, sub, ps, pb)
            qk_norm_s(ps, pb, kT[:, sub, 0:NTK], b_kT, col("kn"), kf_ap=kf[:, sub, 0:NTK])
        ps, pb = get_mm()
        pe_group(PE, [(lambda kc=kc: nc.tensor.matmul(ps[0:NTK, 0:KVW], lhsT=XT[:, kc, 0:NTK],
                                                      rhs=Wp[:, kc, 256:512], start=(kc == 0), stop=(kc == 15)))
                      for kc in range(16)], reads=[bW, b_XT], writes=[pb])
        op(ACT, lambda ps=ps: nc.scalar.copy(vB[0:NTK, 0, :], ps[0:NTK, 0:KVW]), reads=[pb], writes=[b_vB])
        og, ob = get_ostg()
        op(ACT, lambda ps=ps, og=og: nc.scalar.copy(og[0:NTK, 0:KVW], ps[0:NTK, 0:KVW]), reads=[pb], writes=[ob])
        for b in range(NSMP):
            dma(SP, osm[rr['osl']], nvs[b, 128 - SD:128, :], og[b * SD:(b + 1) * SD, 0:KVW], reads=[ob])
        if _DBG['stop'] <= 4:
            return
        pe_group(PE, [(lambda s2=s2: nc.tensor.transpose(pf[0:NTK, 5, s2 * 128:(s2 + 1) * 128], kf[:, s2, 0:NTK],
                                                          ident_f[:, :])) for s2 in range(2)],
                 reads=[b_kf, b_const], writes=[bank[5]])
        og, ob = get_ostg()
        op(ACT, lambda og=og: nc.scalar.copy(og[0:NTK, 0:KVW], pf[0:NTK, 5, 0:KVW]), reads=[bank[5]], writes=[ob])
        for b in range(NSMP):
            dma(SP, osm[rr['osl']], nks[b, 128 - SD:128, :], og[b * SD:(b + 1) * SD, 0:KVW], reads=[ob])
        for i in range(4):
            Wp, bW = panel(gbase + P_AG[i])
            for s2 in range(2):
                c = 2 * i + s2
                ps, pb = get_mm()
                wgroup_s(Wp, bW, 2 * s2, ps, pb)
                op(ACT, lambda c=c, ps=ps: nc.scalar.copy(uS[:, c, :, CB:CB + SD],
                                                          ps[:, 0:NTK].rearrange("p (b i) -> p b i", i=SD)),
                   reads=[pb], writes=[b_u[c]])
                ps, pb = get_mm()
                wgroup_s(Wp, bW, 2 * s2 + 1, ps, pb)
                sg, sgb = get_t32()
                op(ACT, lambda ps=ps, sg=sg: nc.scalar.activation(out=sg[:, 0:NTK], in_=ps[:, 0:NTK],
                                                                  func=AF.Sigmoid), reads=[pb], writes=[sgb])
                op(POOL, lambda c=c, sg=sg: nc.gpsimd.tensor_tensor(
                    out=uS[:, c, :, CB:CB + SD], in0=uS[:, c, :, CB:CB + SD],
                    in1=sg[:, 0:NTK].rearrange("p (b i) -> p b i", i=SD), op=ALU.mult),
                   reads=[sgb], writes=[b_u[c]])
                conv_chunk_s(c)
        un_, unb = get_t32()
        op(POOL, lambda: nc.gpsimd.tensor_copy(un_.rearrange("p (c b i) -> p c b i", c=8, i=SD),
                                               uS[:, :, :, CB:CB + SD]), reads=b_u, writes=[unb])
        for hh in range(2):
            pe_group(PE, [(lambda c4=c4: nc.tensor.transpose(
                pf[0:NTK, 2 + hh, c4 * 128:(c4 + 1) * 128], un_[:, (hh * 4 + c4) * NTK:(hh * 4 + c4 + 1) * NTK],
                ident_f[:, :])) for c4 in range(4)], reads=[unb, b_const], writes=[bank[2 + hh]])
            og, ob = get_ostg()
            op(ACT, lambda og=og, hh=hh: nc.scalar.copy(og[0:NTK, :], pf[0:NTK, 2 + hh, :]), reads=[bank[2 + hh]],
               writes=[ob])
            for b in range(NSMP):
                dma(SP, osm[rr['osl']], ncs[b, CB - SD:CB, hh * 512:(hh + 1) * 512], og[b * SD:(b + 1) * SD, :],
                    reads=[ob])

        if _DBG['stop'] <= 5:
            return
        pTb = [x1b2[:, 0:1024].rearrange("p (h a n) -> p h a n", h=2, a=2),
               x1b2[:, 2048:3072].rearrange("p (h a n) -> p h a n", h=2, a=2)]
        pe_group(PE, [(lambda hf=hf, pr=pr: nc.tensor.matmul(
            pf[0:NTK, hf, pr * 256:(pr + 1) * 256].rearrange("p (g q) -> p g q", g=4),
            lhsT=kT[hf * 64:(hf + 1) * 64, pr, 0:NTK],
            rhs=qT[hf * 64:(hf + 1) * 64, pr * 4:pr * 4 + 4, 0:NTK], start=True, stop=True))
            for pr in range(2) for hf in range(2)], reads=[b_kT, b_qT], writes=[bank[0], bank[1]])
        for hf in range(2):
            op(ACT, lambda hf=hf: nc.scalar.activation(
                out=pTn[:, hf, :, :], in_=pf[0:NTK, hf, :].rearrange("p (a n) -> p a n", a=2), func=AF.Exp,
                scale=HD ** -0.5), reads=[bank[hf]], writes=[b_scr2])
        op(POOL, lambda: nc.gpsimd.tensor_tensor(
            out=pTn.rearrange("p h a (g q) -> p (h a g) q", g=4), in0=pTn.rearrange("p h a (g q) -> p (h a g) q", g=4),
            in1=maskN[:, :].unsqueeze(1).to_broadcast([NTK, 16, NTK]), op=ALU.mult),
           reads=[b_scr2, b_const], writes=[b_scr2])
        fns = []
        for kvh in range(4):
            hf, pr = kvh % 2, kvh // 2
            fns.append(lambda hf=hf, pr=pr, kvh=kvh: nc.tensor.matmul(
                pf[hf * 64:(hf + 1) * 64, 4 + pr, 0:256],
                lhsT=vB[0:NTK, 0, kvh * 64:(kvh + 1) * 64], rhs=pTn[:, hf, pr, :], start=True, stop=False))
            fns.append(lambda hf=hf, pr=pr: nc.tensor.matmul(
                pf[hf * 64:(hf + 1) * 64, pr, 0:256],
                lhsT=ones_b[0:NTK, :], rhs=pTn[:, hf, pr, :], start=True, stop=False))
        pe_group(PE, fns, reads=[b_vB, b_scr2, b_const], writes=[bank[4], bank[5], bank[0], bank[1]])
        b_pTb = [bf("pTb0"), bf("pTb1")]

        def qk_s(b):
            sl = b % 2
            pe_group(PE, [(lambda hf=hf, pr=pr: nc.tensor.matmul(
                pf[:, 2 + hf, pr * 256:(pr + 1) * 256].rearrange("p (g q) -> p g q", g=4),
                lhsT=kcT[hf * 64:(hf + 1) * 64, b, pr, :],
                rhs=qT[hf * 64:(hf + 1) * 64, pr * 4:pr * 4 + 4, 0:NTK], start=True, stop=True))
                for pr in range(2) for hf in range(2)], reads=b_PH + [b_qT], writes=[bank[2], bank[3]])
            for hf in range(2):
                op(ACT, lambda hf=hf, sl=sl: nc.scalar.activation(
                    out=pTb[sl][:, hf, :, :], in_=pf[:, 2 + hf, :].rearrange("p (a n) -> p a n", a=2),
                    func=AF.Exp, scale=HD ** -0.5), reads=[bank[2 + hf]], writes=[b_pTb[sl], b_scr2])
            op(DVE, lambda b=b, sl=sl: nc.vector.tensor_tensor(
                out=pTb[sl].rearrange("p h a (g q) -> p (h a g) q", g=4),
                in0=pTb[sl].rearrange("p h a (g q) -> p (h a g) q", g=4),
                in1=maskCA[:, b, :].unsqueeze(1).to_broadcast([128, 16, NTK]), op=ALU.mult),
               reads=[b_pTb[sl], b_const], writes=[b_pTb[sl]])

        def pv_s(b):
            sl = b % 2
            fns = []
            lastb = (b == NSMP - 1)
            for kvh in range(4):
                hf, pr = kvh % 2, kvh // 2
                fns.append(lambda hf=hf, pr=pr, kvh=kvh, b=b, sl=sl: nc.tensor.matmul(
                    pf[hf * 64:(hf + 1) * 64, 4 + pr, 0:256],
                    lhsT=cvB[:, b, kvh * 64:(kvh + 1) * 64], rhs=pTb[sl][:, hf, pr, :], start=False, stop=lastb))
                fns.append(lambda hf=hf, pr=pr, sl=sl: nc.tensor.matmul(
                    pf[hf * 64:(hf + 1) * 64, pr, 0:256],
                    lhsT=ones_b[:, :], rhs=pTb[sl][:, hf, pr, :], start=False, stop=lastb))
            pe_group(PE, fns, reads=[b_pTb[sl], b_const] + b_PH, writes=[bank[4], bank[5], bank[0], bank[1]])

        qk_s(0)
        for b in range(NSMP):
            if b + 1 < NSMP:
                qk_s(b + 1)
            pv_s(b)
        for pr in range(2):
            dn_, dnb = get_t32()
            dn = dn_[:, 0:256]
            op(DVE, lambda pr=pr, dn=dn: nc.vector.tensor_tensor(
                out=dn.rearrange("p (g q) -> p g q", g=4), in0=pf[:, pr, 0:256].rearrange("p (g q) -> p g q", g=4),
                in1=ES[:, pr, :, 0:NTK], op=ALU.add), reads=[bank[pr], b_const], writes=[dnb])
            op(DVE, lambda dn=dn: nc.vector.reciprocal(dn, dn), reads=[dnb], writes=[dnb])
            op(DVE, lambda pr=pr, dn=dn: nc.vector.tensor_tensor(
                out=MH[:, pr * 4:pr * 4 + 4, 0:NTK], in0=pf[:, 4 + pr, 0:256].rearrange("p (g q) -> p g q", g=4),
                in1=dn.rearrange("p (g q) -> p g q", g=4), op=ALU.mult), reads=[bank[4 + pr], dnb], writes=b_MHb)
        if _DBG['stop'] <= 6:
            return
        def yc(c):
            return x1[:, 1, c * NTK:(c + 1) * NTK]
        for c in range(8):
            yb, ybb = get_t16()
            yq, yqb = get_t16()
            op(ACT, lambda c=c, yb=yb: nc.scalar.activation(out=yb[:, 0:NTK], in_=yc(c), func=AF.Identity,
                                                           bias=col("bdw", c)), reads=[b_x1[1], b_const],
               writes=[ybb])
            op(ACT, lambda c=c, yq=yq: nc.scalar.activation(out=yq[:, 0:NTK], in_=yc(c), func=AF.Square,
                                                           bias=col("bdw", c)), reads=[b_x1[1], b_const],
               writes=[yqb])
            pe_group(PE, [lambda c=c, yb=yb: nc.tensor.matmul(pf[:, 2, 0:NTK], lhsT=onesN[:, :], rhs=yb[:, 0:NTK],
                                                              start=(c == 0), stop=(c == 7)),
                          lambda c=c, yq=yq: nc.tensor.matmul(pf[:, 3, 0:NTK], lhsT=onesN[:, :], rhs=yq[:, 0:NTK],
                                                              start=(c == 0), stop=(c == 7))],
                     reads=[ybb, yqb, b_const], writes=[bank[2], bank[3]])
        mean, mb_ = t32[:, NT32 - 2, 0:NTK], b_t32[NT32 - 2]
        var, vb_ = t32[:, NT32 - 1, 0:NTK], b_t32[NT32 - 1]
        op(ACT, lambda: nc.scalar.copy(mean, pf[:, 2, 0:NTK]), reads=[bank[2]], writes=[mb_])
        op(DVE, lambda: nc.vector.tensor_tensor(out=var, in0=mean, in1=mean, op=ALU.mult), reads=[mb_],
           writes=[vb_])
        op(DVE, lambda: nc.vector.tensor_tensor(out=var, in0=pf[:, 3, 0:NTK], in1=var, op=ALU.subtract),
           reads=[bank[3], vb_], writes=[vb_])
        op(ACT, lambda: nc.scalar.activation(out=var, in_=var, func=AF.Sqrt, bias=eps_c[:, :]),
           reads=[vb_, b_const], writes=[vb_])
        op(DVE, lambda: nc.vector.reciprocal(var, var), reads=[vb_], writes=[vb_])
        for c in range(8):
            t1_, t1b = get_t32()
            t1 = t1_[:, 0:NTK]
            op(DVE, lambda c=c, t1=t1: nc.vector.scalar_tensor_tensor(
                out=t1, in0=yc(c), scalar=col("bdw", c), in1=mean, op0=ALU.add, op1=ALU.subtract),
               reads=[b_x1[1], mb_, b_const], writes=[t1b])
            op(DVE, lambda t1=t1: nc.vector.tensor_tensor(out=t1, in0=t1, in1=var, op=ALU.mult),
               reads=[t1b, vb_], writes=[t1b])
            op(ACT, lambda c=c, t1=t1: nc.scalar.activation(out=MH[:, 8 + c, 0:NTK], in_=t1, func=AF.Silu,
                                                           bias=col("bln", c), scale=col("gln", c)),
               reads=[t1b, b_const], writes=b_MHb)

        if _DBG.get("dump"):
            dma(POOL, osm_misc, dbg_out, MH[:, :, 0:NTK], reads=b_MHb)
        if _DBG['stop'] <= 7:
            return
        for n in range(4):
            Wp, bW = panel(gbase + P_WO + n)
            ps, pb = get_mm()
            pe_group(PE, [(lambda kc=kc: nc.tensor.matmul(ps[0:NTK, :], lhsT=MH[:, kc, 0:NTK], rhs=Wp[:, kc, :],
                                                          start=(kc == 0), stop=(kc == 15))) for kc in range(16)],
                     reads=[bW] + b_MHb, writes=[pb])
            op(DVE, lambda n=n, ps=ps: nc.vector.tensor_tensor(
                out=x1[0:NTK, 0, n * 512:(n + 1) * 512], in0=ps[0:NTK, :], in1=x1[0:NTK, 0, n * 512:(n + 1) * 512],
                op=ALU.add), reads=[pb], writes=[b_x1[0]])
        norm_transpose(x1[0:NTK, 0, :], b_x1[0], NTK, MH, b_MHb, 0, "gmlp")

        hff = PH[:, :].rearrange("p (f t) -> p f t", t=T)
        for qt in range(4):
            for i in range(4):
                Wp, bW = panel(gbase + P_FF + qt * 8 + i)
                for sub in range(4):
                    ps, pb = get_mm()
                    pe_group(PE, [(lambda kc=kc: nc.tensor.matmul(ps[:, 0:NTK],
                                                                  lhsT=Wp[:, kc, sub * 128:(sub + 1) * 128],
                                                                  rhs=MH[:, kc, 0:NTK], start=(kc == 0),
                                                                  stop=(kc == 15))) for kc in range(16)],
                             reads=[bW] + b_MHb, writes=[pb])
                    r_, rb = get_t32()
                    op(ACT, lambda ps=ps, r_=r_: nc.scalar.activation(out=r_[:, 0:NTK], in_=ps[:, 0:NTK],
                                                                      func=AF.Relu), reads=[pb], writes=[rb])
                    f = i * 4 + sub
                    op(ACT, lambda r_=r_, f=f: nc.scalar.activation(out=hff[:, f, 0:NTK], in_=r_[:, 0:NTK],
                                                                    func=AF.Square), reads=[rb], writes=b_PH)
            for n in range(4):
                Wp, bW = panel(gbase + P_FF + qt * 8 + 4 + n)
                ps, pb = pf[:, 2 + n % 2, :], bank[2 + n % 2]
                pe_group(PE, [(lambda kc=kc: nc.tensor.matmul(ps[0:NTK, :], lhsT=hff[:, kc, 0:NTK],
                                                              rhs=Wp[:, kc, :], start=(kc == 0), stop=(kc == 15)))
                              for kc in range(16)], reads=[bW] + b_PH, writes=[pb])
                op(DVE, lambda n=n, ps=ps: nc.vector.tensor_tensor(
                    out=x1[0:NTK, 0, n * 512:(n + 1) * 512], in0=ps[0:NTK, :],
                    in1=x1[0:NTK, 0, n * 512:(n + 1) * 512], op=ALU.add), reads=[pb], writes=[b_x1[0]])
        dma(SP, ysem[0], ys[:, :], x1[0:NTK, 0, :], reads=[b_x1[0]])

    ntiles = NSEQ * (S // T) + 1
    wstate["total"] = ntiles * NPANEL
    ti = 0
    if _DBG["prompt"]:
        tl = [(seq, j) for seq in range(NSEQ) for j in range(S // T)]
        prompt_phase_a(*tl[0])
        for k, (seq, j) in enumerate(tl):
            nxt = prompt_phase_a_steps(*tl[k + 1]) if k + 1 < len(tl) else None
            prompt_tile(ti, seq, j, ti * NPANEL, next_a=nxt)
            assert not nxt
            ti += 1
    if _DBG["sample"]:
        sample_tile(ti * NPANEL)

    for d in out_sems:
        if d.cnt > 0:
            nc.sync.wait_ge(d.sem, d.cnt)
    es.close()
    return nc


_NC_CACHE = {}


def kernel(x_prompt, x_sample, cache_k, cache_v, state_conv, g_mix_norm, w_in, q_norm, k_norm, sinks,
           w_dw, b_dw, g_conv_ln, b_conv_ln, w_out, g_mlp_norm, w_up, w_down):
    f = lambda a: np.ascontiguousarray(np.asarray(a, dtype=np.float32))
    x_prompt = f(x_prompt)
    x_sample = f(x_sample)
    cache_k = f(cache_k)
    cache_v = f(cache_v)
    state_conv = f(state_conv)
    shared = {
        "g_mix": f(g_mix_norm)[0], "w_in": f(w_in)[0], "q_norm": f(q_norm)[0], "k_norm": f(k_norm)[0],
        "sinks": f(sinks)[0], "w_dw": f(w_dw)[0], "b_dw": f(b_dw)[0], "g_ln": f(g_conv_ln)[0],
        "b_ln": f(b_conv_ln)[0], "w_out": f(w_out)[0], "g_mlp": f(g_mlp_norm)[0], "w_up": f(w_up)[0],
        "w_down": f(w_down)[0],
    }
    in_maps = []
    for c in range(NCORES):
        m = dict(shared)
        m["xp"] = x_prompt[NSEQ * c:NSEQ * (c + 1)].reshape(NSEQ * S, D)
        m["xs"] = x_sample[NSMP * c:NSMP * (c + 1)].reshape(NS_TOK, D)
        m["ck"] = cache_k[0, NSMP * c:NSMP * (c + 1)].reshape(NSMP, 128, KVW)
        m["cv"] = cache_v[0, NSMP * c:NSMP * (c + 1)].reshape(NSMP, 128, KVW)
        m["sc"] = state_conv[0, NSMP * c:NSMP * (c + 1)]
        in_maps.append(m)
    if "nc" not in _NC_CACHE:
        _NC_CACHE["nc"] = build_program()
    nc = _NC_CACHE["nc"]
    res = run_bass_kernel_spmd(nc, in_maps, core_ids=list(range(NCORES)))
    R = res.results
    cat = lambda k: np.concatenate([np.asarray(r[k]) for r in R], axis=0)
    y_prompt = cat("yp").reshape(16, S, D)
    y_sample = cat("ys").reshape(128, SD, D)
    nk_p = cat("nkp").reshape(1, 16, 128, 4, HD)
    nv_p = cat("nvp").reshape(1, 16, 128, 4, HD)
    nc_p = cat("ncp").reshape(1, 16, CB, CCH)
    nk_s = cat("nks").reshape(1, 128, 128, 4, HD)
    nv_s = cat("nvs").reshape(1, 128, 128, 4, HD)
    nc_s = cat("ncs").reshape(1, 128, CB, CCH)
    return (y_prompt.astype(np.float32), y_sample.astype(np.float32), nk_p.astype(np.float32),
            nv_p.astype(np.float32), nc_p.astype(np.float32), nk_s.astype(np.float32),
            nv_s.astype(np.float32), nc_s.astype(np.float32))
```

```python
import numpy as np
from contextlib import ExitStack
import concourse.bass as bass
import concourse.mybir as mybir
from concourse.bass_utils import run_bass_kernel_spmd

F32 = mybir.dt.float32
BF16 = mybir.dt.bfloat16
AF = mybir.ActivationFunctionType
ALU = mybir.AluOpType

NCORES = 8
D = 2048
S = 2048
HD = 64
KVW = 256
CCH = 1024
INW = 3584
DFF = 8192
T = 512
CW = 31
CB = 30
NSEQ = 2
NSMP = 16
SD = 4
NS_TOK = NSMP * SD
EPS = 1e-6
NPANEL = 47
P_Q = 0
P_KV = 2
P_AG = [3, 4, 5, 6]
P_DG = [7, 8, 9, 10]
P_WO = 11
P_FF = 15
NSLOT = 3
PW = 16 * 512


class DSem:
    def __init__(self, sem):
        self.sem = sem
        self.cnt = 0


class Buf:
    def __init__(self, name=""):
        self.name = name
        self.w = {}
        self.r = {}


class Q:
    def __init__(self, name, eng, sem, is_pe=False):
        self.name = name
        self.eng = eng
        self.sem = sem
        self.cnt = 0
        self.seen = {}
        self.is_pe = is_pe

    def need(self, sem, val):
        if sem is self.sem or sem == self.sem:
            if self.is_pe:
                return
            if self.cnt + 1 - val >= 3:
                return
        if self.seen.get(sem, 0) >= val:
            return
        self.eng.wait_ge(sem, val)
        self.seen[sem] = val


def _merge(d, e):
    for k, v in e.items():
        if d.get(k, 0) < v:
            d[k] = v


def _deps(reads, writes):
    d = {}
    for b in reads:
        _merge(d, b.w)
    for b in writes:
        _merge(d, b.w)
        _merge(d, b.r)
    return d


def _record(ev_sem, ev_val, reads, writes):
    for b in reads:
        if b.r.get(ev_sem, 0) < ev_val:
            b.r[ev_sem] = ev_val
    for b in writes:
        if b.w.get(ev_sem, 0) < ev_val:
            b.w[ev_sem] = ev_val


def op(q, fn, reads=(), writes=()):
    for s, v in _deps(reads, writes).items():
        q.need(s, v)
    inst = fn()
    q.cnt += 1
    inst.then_inc(q.sem, 1)
    _record(q.sem, q.cnt, reads, writes)


def pe_group(q, fns, reads=(), writes=()):
    for s, v in _deps(reads, writes).items():
        q.need(s, v)
    inst = None
    for f in fns:
        inst = f()
    q.cnt += 1
    inst.then_inc(q.sem, 1)
    _record(q.sem, q.cnt, reads, writes)


def dma(q, dsem, out, in_, reads=(), writes=()):
    for s, v in _deps(reads, writes).items():
        q.need(s, v)
    inst = q.eng.dma_start(out=out, in_=in_)
    dsem.cnt += 16
    inst.then_inc(dsem.sem, 16)
    _record(dsem.sem, dsem.cnt, reads, writes)


_DBG = {"prompt": True, "sample": True, "stop": 99}


def build_program():
    nc = bass.Bass("TRN2", target_bir_lowering=False)
    es = ExitStack()

    def din(name, shape):
        return nc.dram_tensor(name, list(shape), F32, kind="ExternalInput").ap()

    def dout(name, shape):
        return nc.dram_tensor(name, list(shape), F32, kind="ExternalOutput").ap()

    xp = din("xp", [NSEQ * S, D])
    xs = din("xs", [NS_TOK, D])
    ck = din("ck", [NSMP, 128, KVW])
    cv = din("cv", [NSMP, 128, KVW])
    sc = din("sc", [NSMP, CB, CCH])
    g_mix = din("g_mix", [D])
    w_in = din("w_in", [D, INW])
    q_norm = din("q_norm", [HD])
    k_norm = din("k_norm", [HD])
    sinks = din("sinks", [16])
    w_dw = din("w_dw", [CW, CCH])
    b_dw = din("b_dw", [CCH])
    g_ln = din("g_ln", [CCH])
    b_ln = din("b_ln", [CCH])
    w_out = din("w_out", [D, D])
    g_mlp = din("g_mlp", [D])
    w_up = din("w_up", [D, DFF])
    w_down = din("w_down", [DFF, D])

    yp = dout("yp", [NSEQ * S, D])
    ys = dout("ys", [NS_TOK, D])
    nkp = dout("nkp", [NSEQ, 128, KVW])
    nvp = dout("nvp", [NSEQ, 128, KVW])
    ncp = dout("ncp", [NSEQ, CB, CCH])
    nks = dout("nks", [NSMP, 128, KVW])
    nvs = dout("nvs", [NSMP, 128, KVW])
    ncs = dout("ncs", [NSMP, CB, CCH])

    wscr = nc.dram_tensor("wscr", [NPANEL, 128, PW], BF16, kind="Internal").ap()
    dbg_out = nc.dram_tensor("dbg", [128, 16, NS_TOK], F32, kind="ExternalOutput").ap() if _DBG.get("dump") else None

    def sb(name, shape, dt):
        return es.enter_context(nc.sbuf_tensor(name, list(shape), dt))

    def newsem(name):
        return es.enter_context(nc.semaphore(name))

    PE = Q("pe", nc.tensor, newsem("s_pe"), is_pe=True)
    ACT = Q("act", nc.scalar, newsem("s_act"))
    DVE = Q("dve", nc.vector, newsem("s_dve"))
    POOL = Q("pool", nc.gpsimd, newsem("s_pool"))
    SP = Q("sp", nc.sync, newsem("s_sp"))

    out_sems = []

    def osem(name):
        d = DSem(newsem(name))
        out_sems.append(d)
        return d

    xin = sb("xin", [128, D], F32)
    xhat = sb("xhat", [128, D], BF16)
    XT = sb("XT", [128, 16, T], BF16)
    MH = sb("MH", [128, 16, T], BF16)
    qT = sb("qT", [128, 8, T], BF16)
    kT = sb("kT", [128, 2, 128 + T], BF16)
    vB = sb("vB", [128, 5, KVW], BF16)
    UY = sb("UY", [128, 8 * (CB + T + 2)], F32)
    UYb = UY[:, :].bitcast(BF16)
    uTb = UYb[:, 0:8 * (CB + T + 2)].rearrange("p (c t) -> p c t", c=8)
    ybf = UYb[:, 8 * (CB + T + 2):8 * (CB + T + 2) + 8 * T].rearrange("p (c t) -> p c t", c=8)
    uhalo = sb("uhalo", [128, 8, CB], F32)
    PH = sb("PH", [128, 2 * 4096], BF16)
    x1 = sb("x1", [128, 4, D], F32)
    W = sb("W", [128, NSLOT, PW], BF16)
    NT32 = 8
    t32 = sb("t32", [128, NT32, T], F32)
    NT16 = 4
    t16 = sb("t16", [128, NT16, T], BF16)
    io = sb("io", [128, 128], F32)
    ident_f = sb("ident_f", [128, 128], F32)
    ident_b = sb("ident_b", [128, 128], BF16)
    masks = sb("masks", [128, 2, 128], BF16)
    ones_b = sb("ones_b", [128, 64], BF16)
    blk1 = sb("blk1", [128, 128], BF16)
    onesN = sb("onesN", [128, 128], BF16)
    eps_c = sb("eps_c", [128, 1], F32)
    vrows = sb("vrows", [64, 128], F32)
    wrows = sb("wrows", [128, 2, 128], F32)
    cols = sb("cols", [128, 64], F32)
    wdwc = sb("wdwc", [128, 248], F32)
    es16 = sb("es16", [128, 16], F32)
    ES = sb("ES", [128, 2, 4, 128], F32)
    stat = sb("stat", [128, 8, 4], F32)
    Eblk = sb("Eblk", [16, 64], BF16)
    maskN = sb("maskN", [64, 64], BF16)
    maskCA = sb("maskCA", [128, NSMP, NS_TOK], BF16)
    ostg = sb("ostg", [128, 2, 512], F32)
    kf = sb("kf", [128, 2, 128], F32)

    pf = es.enter_context(nc.psum_tensor("pf", [128, 6, 512], F32))
    trp = es.enter_context(nc.psum_tensor("trp", [128, 16, 128], BF16))
    trf = trp[:, :, :].rearrange("p a b -> p (a b)").bitcast(F32).rearrange("p (k n) -> p k n", k=2)

    B = {}

    def bf(name):
        if name not in B:
            B[name] = Buf(name)
        return B[name]

    bank = [bf(f"bank{i}") for i in range(6)]
    b_tr = bf("trp")
    b_W = [bf(f"W{i}") for i in range(NSLOT)]
    b_panel = [bf(f"panel{i}") for i in range(NPANEL)]
    b_x1 = [bf(f"x1_{i}") for i in range(4)]
    b_t32 = [bf(f"t32_{i}") for i in range(NT32)]
    b_t16 = [bf(f"t16_{i}") for i in range(NT16)]
    b_u = [bf(f"u{i}") for i in range(8)]
    b_yc = [bf(f"y{i}") for i in range(8)]
    b_stat = [bf(f"stat{i}") for i in range(8)]
    b_ostg = [bf("ostg0"), bf("ostg1")]
    b_const = bf("const")

    rr = {"t32": 0, "t16": 0, "mm": 0, "stat": 0, "ostg": 0, "osl": 0, "mm4": 0}

    def get_t32():
        i = rr["t32"] % (NT32 - 2)
        rr["t32"] += 1
        return t32[:, i, :], b_t32[i]

    def get_t16():
        i = rr["t16"] % NT16
        rr["t16"] += 1
        return t16[:, i, :], b_t16[i]

    def get_mm():
        i = rr["mm"] % 2
        rr["mm"] += 1
        return pf[:, i, :], bank[i]

    def get_mm4():
        i = rr["mm4"] % 4
        rr["mm4"] += 1
        return pf[:, i, :], bank[i]

    def get_stat():
        i = rr["stat"] % 8
        rr["stat"] += 1
        return stat[:, i, :], b_stat[i]

    def get_ostg():
        i = rr["ostg"] % 2
        rr["ostg"] += 1
        rr["osl"] = i
        return ostg[:, i, :], b_ostg[i]

    csem = DSem(newsem("d_const"))
    op(POOL, lambda: nc.gpsimd.iota(io[:], [[1, 128]], base=0, channel_multiplier=-1,
                                    allow_small_or_imprecise_dtypes=True), writes=[bf("io")])
    op(DVE, lambda: nc.vector.tensor_single_scalar(ident_f[:], io[:], 0.0, op=ALU.is_equal),
       reads=[bf("io")], writes=[b_const])
    op(DVE, lambda: nc.vector.tensor_single_scalar(masks[:, 0, :], io[:], 0.0, op=ALU.is_lt),
       reads=[bf("io")], writes=[b_const])
    op(DVE, lambda: nc.vector.tensor_single_scalar(masks[:, 1, :], io[:], 0.0, op=ALU.is_ge),
       reads=[bf("io")], writes=[b_const])
    op(DVE, lambda: nc.vector.memset(ones_b[:], 1.0), writes=[b_const])
    op(DVE, lambda: nc.vector.memset(blk1[:], 0.0), writes=[b_const])
    op(DVE, lambda: nc.vector.memset(onesN[:], 1.0 / CCH), writes=[b_const])
    op(DVE, lambda: nc.vector.memset(eps_c[:], EPS), writes=[b_const])
    op(DVE, lambda: nc.vector.tensor_copy(ident_b[:], ident_f[:]), reads=[b_const], writes=[b_const])
    op(DVE, lambda: nc.vector.memset(blk1[0:64, 0:64], 1.0 / HD), reads=[b_const], writes=[b_const])
    op(DVE, lambda: nc.vector.memset(blk1[64:128, 64:128], 1.0 / HD), reads=[b_const], writes=[b_const])

    b_rows = bf("rows")
    row_srcs = [(g_mix, 16), (g_mlp, 16), (b_dw, 8), (g_ln, 8), (b_ln, 8)]
    r0 = 0
    COL = {}
    for nm, (apv, n) in zip(["gmix", "gmlp", "bdw", "gln", "bln"], row_srcs):
        dma(SP, csem, vrows[r0:r0 + n, :], apv.rearrange("(r c) -> r c", c=128), writes=[b_rows])
        COL[nm] = r0
        r0 += n
    COL["qn"] = r0
    for h in range(2):
        dma(SP, csem, vrows[r0:r0 + 1, h * 64:(h + 1) * 64], q_norm.rearrange("(r c) -> r c", c=64),
            writes=[b_rows])
    r0 += 1
    COL["kn"] = r0
    for h in range(2):
        dma(SP, csem, vrows[r0:r0 + 1, h * 64:(h + 1) * 64], k_norm.rearrange("(r c) -> r c", c=64),
            writes=[b_rows])
    r0 += 1
    NR = r0
    wdw_rows = w_dw.rearrange("j (c x) -> (j c) x", x=128)
    dma(SP, csem, wrows[:, 0, :], wdw_rows[0:128, :], writes=[b_rows])
    dma(SP, csem, wrows[0:120, 1, :], wdw_rows[128:248, :], writes=[b_rows])
    dma(SP, csem, es16[:], sinks.partition_broadcast(128), writes=[b_rows])
    b_rows.w = {csem.sem: csem.cnt}

    pe_group(PE, [lambda: nc.tensor.transpose(pf[:, 2, 0:NR], vrows[0:NR, :], ident_f[0:NR, 0:NR])],
             reads=[b_rows, b_const], writes=[bank[2]])
    op(DVE, lambda: nc.vector.tensor_copy(cols[:, 0:NR], pf[:, 2, 0:NR]), reads=[bank[2]], writes=[b_const])
    pe_group(PE, [lambda: nc.tensor.transpose(pf[:, 3, 0:128], wrows[:, 0, :], ident_f[:, :]),
                  lambda: nc.tensor.transpose(pf[:, 3, 128:248], wrows[0:120, 1, :], ident_f[0:120, 0:120])],
             reads=[b_rows, b_const], writes=[bank[3]])
    op(DVE, lambda: nc.vector.tensor_copy(wdwc[:, :], pf[:, 3, 0:248]), reads=[bank[3]], writes=[b_const])
    op(ACT, lambda: nc.scalar.activation(out=es16[:], in_=es16[:], func=AF.Exp), reads=[b_rows], writes=[b_rows])
    for pr in range(2):
        for hf in range(2):
            for g in range(4):
                hh = (2 * pr + hf) * 4 + g
                op(DVE, lambda pr=pr, hf=hf, g=g, hh=hh: nc.vector.tensor_copy(
                    ES[hf * 64:(hf + 1) * 64, pr, g, :],
                    es16[hf * 64:(hf + 1) * 64, hh:hh + 1].to_broadcast([64, 128])),
                   reads=[b_rows], writes=[b_const])

    op(POOL, lambda: nc.gpsimd.iota(io[0:16, 0:64], [[1, 64]], base=0, channel_multiplier=-4,
                                    allow_small_or_imprecise_dtypes=True), reads=[b_const], writes=[bf("io")])
    op(DVE, lambda: nc.vector.tensor_single_scalar(wrows[0:16, 0, 0:64], io[0:16, 0:64], 0.0, op=ALU.is_ge),
       reads=[bf("io"), bank[3]], writes=[b_rows])
    op(DVE, lambda: nc.vector.tensor_single_scalar(wrows[0:16, 1, 0:64], io[0:16, 0:64], 3.0, op=ALU.is_le),
       reads=[bf("io"), bank[3]], writes=[b_rows])
    op(DVE, lambda: nc.vector.tensor_tensor(out=Eblk[:, :], in0=wrows[0:16, 0, 0:64], in1=wrows[0:16, 1, 0:64],
                                            op=ALU.mult), reads=[b_rows], writes=[b_const])
    pe_group(PE, [lambda: nc.tensor.matmul(pf[0:64, 2, 0:64], lhsT=Eblk[:, :], rhs=Eblk[:, :], start=True,
                                           stop=True)], reads=[b_const], writes=[bank[2]])
    op(DVE, lambda: nc.vector.tensor_tensor(out=maskN[:, :], in0=pf[0:64, 2, 0:64], in1=masks[0:64, 1, 0:64],
                                            op=ALU.mult), reads=[bank[2], b_const], writes=[b_const])

    op(POOL, lambda: nc.gpsimd.memset(maskCA[:, :, :], 0.0), writes=[b_const])
    for b in range(NSMP):
        op(POOL, lambda b=b: nc.gpsimd.tensor_copy(maskCA[:, b, b * SD:(b + 1) * SD], masks[:, 0, 0:SD]),
           reads=[b_const], writes=[b_const])

    def col(nm, i=0):
        c = COL[nm] + i
        return cols[:, c:c + 1]

    psem = [DSem(newsem(f"d_p{i}")) for i in range(NPANEL)]

    def std_panel(idx, dst, wb, wsrc, r_lo, c_lo):
        src = wsrc[r_lo:r_lo + 2048, c_lo:c_lo + 512].rearrange("(kc p) c -> p kc c", p=128)
        dma(POOL, psem[idx], dst.rearrange("p (kc c) -> p kc c", c=512), src, writes=[wb])

    def q_panel(idx, dst, wb, pr):
        stg = PH[:, :].rearrange("p (kc c) -> p kc c", c=512)
        dma(POOL, psem[idx], stg, w_in[:, 512 * pr:512 * pr + 512].rearrange("(kc p) c -> p kc c", p=128),
            writes=b_PH)
        dst3 = dst.rearrange("p (kc c) -> p kc c", c=512)
        for g in range(4):
            for hf in range(2):
                so = (hf * 4 + g) * 64
                do = g * 128 + hf * 64
                op(POOL, lambda so=so, do=do: nc.gpsimd.tensor_copy(dst3[:, :, do:do + 64], stg[:, :, so:so + 64]),
                   reads=b_PH, writes=[wb])

    def wout_panel(idx, dst, wb, n):
        dst3 = dst.rearrange("p (kc c) -> p kc c", c=512)
        for pr in range(2):
            for hf in range(2):
                rr0 = (2 * pr + hf) * 256
                src = w_out[rr0:rr0 + 256, n * 512:(n + 1) * 512].rearrange("(g d) c -> d g c", g=4)
                dma(POOL, psem[idx], dst3[hf * 64:(hf + 1) * 64, pr * 4:pr * 4 + 4, :], src, writes=[wb])
        src = w_out[1024:2048, n * 512:(n + 1) * 512].rearrange("(kc p) c -> p kc c", p=128)
        dma(POOL, psem[idx], dst3[:, 8:16, :], src, writes=[wb])

    def ag_panel(idx, dst, wb, i):
        dst3 = dst.rearrange("p (kc c) -> p kc c", c=512)
        for s2 in range(2):
            c = 2 * i + s2
            for t2, base in enumerate((1536, 2560)):
                f0 = base + c * 128
                src = w_in[:, f0:f0 + 128].rearrange("(kc p) c -> p kc c", p=128)
                o = (2 * s2 + t2) * 128
                dma(POOL, psem[idx], dst3[:, :, o:o + 128], src, writes=[wb])

    pp_emit = {}
    for i in range(4):
        pp_emit[P_AG[i]] = (lambda dst, wb, i=i: ag_panel(P_AG[i], dst, wb, i))
    pp_emit[P_Q] = lambda dst, wb: q_panel(P_Q, dst, wb, 0)
    pp_emit[P_Q + 1] = lambda dst, wb: q_panel(P_Q + 1, dst, wb, 1)
    pp_emit[P_KV] = lambda dst, wb: std_panel(P_KV, dst, wb, w_in, 0, 1024)
    for n in range(4):
        pp_emit[P_WO + n] = (lambda dst, wb, n=n: wout_panel(P_WO + n, dst, wb, n))
    for qt in range(4):
        for i in range(4):
            pp_emit[P_FF + qt * 8 + i] = (lambda dst, wb, qt=qt, i=i: std_panel(P_FF + qt * 8 + i, dst, wb, w_up, 0,
                                                                                (qt * 4 + i) * 512))
        for n in range(4):
            pp_emit[P_FF + qt * 8 + 4 + n] = (lambda dst, wb, qt=qt, n=n: std_panel(P_FF + qt * 8 + 4 + n, dst, wb,
                                                                                    w_down, qt * 2048, n * 512))

    wsem = [DSem(newsem(f"d_w{i}")) for i in range(NSLOT)]
    wstate = {"next": 0, "total": 0}

    def wload_upto(gidx):
        while wstate["next"] <= min(gidx, wstate["total"] - 1):
            g = wstate["next"]
            slot = g % NSLOT
            idx = g % NPANEL
            if g < NPANEL and idx not in P_DG:
                pp_emit[idx](W[:, slot, :], b_W[slot])
                b_W[slot].w[psem[idx].sem] = psem[idx].cnt
                dma(SP, psem[idx], wscr[idx], W[:, slot, :], reads=[b_W[slot]], writes=[b_panel[idx]])
            else:
                dma(SP, wsem[slot], W[:, slot, :], wscr[idx], reads=[b_panel[idx]], writes=[b_W[slot]])
            wstate["next"] += 1

    def panel(gidx):
        wload_upto(gidx + NSLOT - 1)
        slot = gidx % NSLOT
        return W[:, slot, :].rearrange("p (kc c) -> p kc c", c=512), b_W[slot]

    def panel_flat(gidx):
        wload_upto(gidx + NSLOT - 1)
        slot = gidx % NSLOT
        return W[:, slot, :], b_W[slot]

    def gen_diag():
        for i in range(4):
            if i % 2 == 0:
                stg, sbufs = MH[:, :, :].rearrange("p a b -> p (a b)"), b_MHb
            else:
                stg, sbufs = PH[:, :], b_PH
            for s2 in range(2):
                c = 2 * i + s2
                for j in range(CW):
                    r = s2 * CW + j
                    wc = wdwc[:, j * 8 + c:j * 8 + c + 1]
                    edge = sbufs if (r < 2 or r >= 2 * CW - 2) else []
                    if r % 2 == 0:
                        op(ACT, lambda r=r, wc=wc, stg=stg: nc.scalar.activation(
                            out=stg[:, r * 128:(r + 1) * 128], in_=ident_b[:, :], func=AF.Identity, scale=wc),
                           reads=[b_const], writes=edge)
                    else:
                        op(DVE, lambda r=r, wc=wc, stg=stg: nc.vector.tensor_scalar(
                            out=stg[:, r * 128:(r + 1) * 128], in0=ident_b[:, :], scalar1=wc, scalar2=None,
                            op0=ALU.mult), reads=[b_const], writes=edge)
            op(DVE, lambda stg=stg: nc.vector.memset(stg[:, 2 * CW * 128:PW], 0.0), writes=sbufs)
            dma(SP, psem[P_DG[i]], wscr[P_DG[i]], stg, reads=sbufs, writes=[b_panel[P_DG[i]]])
            b_panel[P_DG[i]].w[psem[P_DG[i]].sem] = psem[P_DG[i]].cnt

    xsem = DSem(newsem("d_xin"))
    xrsem = [DSem(newsem(f"d_xr{i}")) for i in range(4)]
    ysem = [osem(f"d_y{i}") for i in range(4)]
    osm = [osem("d_o0"), osem("d_o1")]
    osm_misc = osem("d_om")
    b_xin = bf("xin")
    b_xhat = bf("xhat")
    b_XT = bf("XT")
    b_MHa = [bf(f"MHa{i}") for i in range(4)]
    b_MHcb = [bf(f"MHc{i}") for i in range(4)]
    b_MHb = b_MHa + b_MHcb
    b_qT = bf("qT")
    b_kT = bf("kT")
    b_vB = bf("vB")
    b_PH = [bf("PH0"), bf("PH1")]
    b_uh = bf("uhalo")
    b_kf = bf("kf")

    def rstd_from_ssq(ssq_ap, ssq_buf, n_feat, npart):
        st, sbuf_ = get_stat()
        op(ACT, lambda: nc.scalar.activation(out=st[:npart, 1:2], in_=ssq_ap, func=AF.Sqrt,
                                             scale=1.0 / n_feat, bias=eps_c[:npart, :]),
           reads=[ssq_buf, b_const], writes=[sbuf_])
        op(DVE, lambda: nc.vector.reciprocal(st[:npart, 2:3], st[:npart, 1:2]), reads=[sbuf_], writes=[sbuf_])
        return st[:npart, 2:3], sbuf_

    def nt_front(src_ap, src_buf, npart):
        st, sbuf_ = get_stat()
        op(ACT, lambda: nc.scalar.activation(out=xhat[:npart, :], in_=src_ap, func=AF.Square,
                                             accum_out=st[:npart, 0:1]),
           reads=[src_buf], writes=[b_xhat, sbuf_])
        rs, rsb = rstd_from_ssq(st[:npart, 0:1], sbuf_, D, npart)
        op(DVE, lambda: nc.vector.tensor_scalar(out=xhat[:npart, :], in0=src_ap, scalar1=rs, scalar2=None,
                                                op0=ALU.mult),
           reads=[src_buf, rsb], writes=[b_xhat])

    def nt_back(npart, dstT, b_dst, tok_off, gname):
        pe_group(PE, [(lambda c=c: nc.tensor.transpose(trp[:, c, 0:npart], xhat[:npart, c * 128:(c + 1) * 128],
                                                        ident_b[:npart, :npart])) for c in range(16)],
                 reads=[b_xhat, b_const], writes=[b_tr])
        g0 = COL[gname]
        op(DVE, lambda: nc.vector.tensor_tensor(
            out=dstT[:, :, tok_off:tok_off + npart], in0=trp[:, :, 0:npart],
            in1=cols[:, g0:g0 + 16].unsqueeze(2).to_broadcast([128, 16, npart]), op=ALU.mult),
           reads=[b_tr, b_const], writes=(b_dst if isinstance(b_dst, list) else [b_dst]))

    def norm_transpose(src_ap, src_buf, npart, dstT, b_dst, tok_off, gname):
        nt_front(src_ap, src_buf, npart)
        nt_back(npart, dstT, b_dst, tok_off, gname)

    def prompt_phase_a_steps(seq, j):
        row0 = seq * S + j * T

        def front(m):
            dma(SP, xsem, xin[:, :], xp[row0 + m * 128:row0 + (m + 1) * 128, :], writes=[b_xin])
            nt_front(xin[:, :], b_xin, 128)

        def back(m):
            nt_back(128, XT, b_XT, m * 128, "gmix")

        steps = [lambda: front(0)]
        for m in range(1, 4):
            steps.append(lambda m=m: (back(m - 1), front(m)))
        steps.append(lambda: back(3))
        return steps

    def prompt_phase_a(seq, j):
        for st_ in prompt_phase_a_steps(seq, j):
            st_()

    def prompt_tile(ti, seq, j, gbase, next_a=None):
        row0 = seq * S + j * T
        first = (j == 0)
        last = (j == S // T - 1)
        NBK = 4

        if first:
            op(POOL, lambda: nc.gpsimd.memset(uTb[:, :, 0:CB], 0.0), writes=b_u)
            op(POOL, lambda: nc.gpsimd.memset(vB[:, 0, :], 0.0), writes=[b_vB])
            op(POOL, lambda: nc.gpsimd.memset(kT[:, :, 0:128], 0.0), writes=[b_kT])
        else:
            op(POOL, lambda: nc.gpsimd.tensor_copy(uTb[:, :, 0:CB], uTb[:, :, T:T + CB]), reads=b_u, writes=b_u)
            op(POOL, lambda: nc.gpsimd.tensor_copy(vB[:, 0, :], vB[:, 4, :]), reads=[b_vB], writes=[b_vB])
            op(POOL, lambda: nc.gpsimd.tensor_copy(kT[:, :, 0:128], kT[:, :, T:T + 128]), reads=[b_kT],
               writes=[b_kT])

        for m in range(NBK):
            dma(SP, xrsem[m], x1[:, m, :], xp[row0 + m * 128:row0 + (m + 1) * 128, :], writes=[b_x1[m]])

        def wgroup(Wp, bW, sub, ps, pbuf, extra_reads=()):
            pe_group(PE, [(lambda kc=kc: nc.tensor.matmul(ps, lhsT=Wp[:, kc, sub * 128:(sub + 1) * 128],
                                                          rhs=XT[:, kc, :], start=(kc == 0), stop=(kc == 15)))
                          for kc in range(16)],
                     reads=[bW, b_XT] + list(extra_reads), writes=[pbuf])

        def ag_groups(i):
            Wp, bW = panel(gbase + P_AG[i])
            for s2 in range(2):
                c = 2 * i + s2
                psA, pbA = get_mm4()
                wgroup(Wp, bW, 2 * s2, psA, pbA)
                psB, pbB = get_mm4()
                wgroup(Wp, bW, 2 * s2 + 1, psB, pbB)
                sg, sgb = get_t32()
                op(ACT, lambda psB=psB, sg=sg: nc.scalar.activation(out=sg, in_=psB, func=AF.Sigmoid), reads=[pbB],
                   writes=[sgb])
                op(DVE, lambda c=c, psA=psA, sg=sg: nc.vector.tensor_tensor(out=uTb[:, c, CB:CB + T], in0=psA,
                                                                           in1=sg, op=ALU.mult),
                   reads=[pbA, sgb], writes=[b_u[c]])
                if last:
                    op(DVE, lambda c=c, psA=psA, sg=sg: nc.vector.tensor_tensor(
                        out=uhalo[:, c, :], in0=psA[:, T - CB:T], in1=sg[:, T - CB:T], op=ALU.mult),
                       reads=[pbA, sgb], writes=[b_uh])

        pend = []

        def ln_stat(c, yq, yqb):
            pe_group(PE, [lambda: nc.tensor.matmul(pf[:, 4, :], lhsT=onesN[:, :], rhs=ybf[:, c, :],
                                                   start=(c == 0), stop=(c == 7)),
                          lambda: nc.tensor.matmul(pf[:, 5, :], lhsT=onesN[:, :], rhs=yq,
                                                   start=(c == 0), stop=(c == 7))],
                     reads=[b_yc[c], yqb, b_const], writes=[bank[4], bank[5]])

        def conv_pe(i):
            Dp, bD = panel_flat(gbase + P_DG[i])
            for s2 in range(2):
                c = 2 * i + s2
                ps, pb = pf[:, c % 4, :], bank[c % 4]
                pe_group(PE, [(lambda j=j: nc.tensor.matmul(
                    ps, lhsT=Dp[:, (s2 * CW + j) * 128:(s2 * CW + j + 1) * 128], rhs=uTb[:, c, j:j + T],
                    start=(j == 0), stop=(j == CW - 1))) for j in range(CW)],
                    reads=[bD, b_u[c]], writes=[pb])
                if pend:
                    ln_stat(*pend.pop())
                yq, yqb = get_t16()
                op(ACT, lambda c=c, ps=ps: nc.scalar.activation(out=ybf[:, c, :], in_=ps, func=AF.Identity,
                                                               bias=col("bdw", c)), reads=[pb, b_const],
                   writes=[b_yc[c]])
                op(ACT, lambda c=c, ps=ps, yq=yq: nc.scalar.activation(out=yq, in_=ps, func=AF.Square,
                                                                      bias=col("bdw", c)), reads=[pb, b_const],
                   writes=[yqb])
                pend.append((c, yq, yqb))
            if i == 3:
                ln_stat(*pend.pop())

        def qk_norm(ps, pb, dst_ap, dst_buf, wcol, kf_ap=None):
            zs, zb = get_t32()
            sq, qb = get_t16()
            op(ACT, lambda: nc.scalar.activation(out=sq, in_=ps, func=AF.Square), reads=[pb], writes=[qb])
            op(ACT, lambda: nc.scalar.copy(zs, ps), reads=[pb], writes=[zb])
            ss, ssb = pf[:, 2 + (rr["mm"] % 2), :], bank[2 + (rr["mm"] % 2)]
            pe_group(PE, [lambda: nc.tensor.matmul(ss, lhsT=blk1[:, :], rhs=sq, start=True, stop=True)],
                     reads=[qb, b_const], writes=[ssb])
            sd_, sdb = get_t32()
            op(ACT, lambda: nc.scalar.activation(out=sd_, in_=ss, func=AF.Sqrt, bias=eps_c[:, :]),
               reads=[ssb, b_const], writes=[sdb])
            op(DVE, lambda: nc.vector.reciprocal(sd_, sd_), reads=[sdb], writes=[sdb])
            op(DVE, lambda: nc.vector.scalar_tensor_tensor(out=dst_ap, in0=zs, scalar=wcol, in1=sd_,
                                                           op0=ALU.mult, op1=ALU.mult),
               reads=[zb, sdb, b_const], writes=[dst_buf])
            if kf_ap is not None:
                op(DVE, lambda: nc.vector.scalar_tensor_tensor(out=kf_ap, in0=zs[:, T - 128:T], scalar=wcol,
                                                               in1=sd_[:, T - 128:T], op0=ALU.mult, op1=ALU.mult),
                   reads=[zb, sdb, b_const], writes=[b_kf])

        for pi in range(2):
            Wp, bW = panel(gbase + P_Q + pi)
            for sub in range(4):
                c = pi * 4 + sub
                ps, pb = get_mm()
                wgroup(Wp, bW, sub, ps, pb)
                qk_norm(ps, pb, qT[:, c, :], b_qT, col("qn"))
        Wp, bW = panel(gbase + P_KV)
        for sub in range(2):
            ps, pb = get_mm()
            wgroup(Wp, bW, sub, ps, pb)
            qk_norm(ps, pb, kT[:, sub, 128:128 + T], b_kT, col("kn"), kf_ap=(kf[:, sub, :] if last else None))
        for m in range(NBK):
            ps, pb = get_mm()
            pe_group(PE, [(lambda kc=kc: nc.tensor.matmul(ps[:, 0:KVW], lhsT=XT[:, kc, m * 128:(m + 1) * 128],
                                                          rhs=Wp[:, kc, 256:512], start=(kc == 0), stop=(kc == 15)))
                          for kc in range(16)], reads=[bW, b_XT], writes=[pb])
            op(ACT, lambda m=m, ps=ps: nc.scalar.copy(vB[:, 1 + m, :], ps[:, 0:KVW]), reads=[pb], writes=[b_vB])
            if last and m == NBK - 1:
                og, ob = get_ostg()
                op(ACT, lambda ps=ps, og=og: nc.scalar.copy(og[:, 0:KVW], ps[:, 0:KVW]), reads=[pb], writes=[ob])
                dma(SP, osm[rr['osl']], nvp[seq], og[:, 0:KVW], reads=[ob])
        if last:
            pe_group(PE, [(lambda s2=s2: nc.tensor.transpose(pf[:, 3, s2 * 128:(s2 + 1) * 128], kf[:, s2, :],
                                                              ident_f[:, :])) for s2 in range(2)],
                     reads=[b_kf, b_const], writes=[bank[3]])
            og, ob = get_ostg()
            op(ACT, lambda og=og: nc.scalar.copy(og[:, 0:KVW], pf[:, 3, 0:KVW]), reads=[bank[3]], writes=[ob])
            dma(SP, osm[rr['osl']], nkp[seq], og[:, 0:KVW], reads=[ob])

        def qk_block(m):
            slot = m % 2
            pT = PH[:, slot * 4096:(slot + 1) * 4096].rearrange("p (ch kv n) -> p ch kv n", ch=2, kv=4)
            chunks = [1] if (first and m == 0) else [0, 1]
            for kvh in range(4):
                hf, pr = kvh % 2, kvh // 2
                for ch in chunks:
                    kb = m + ch
                    sidx = rr["mm4"] % 4
                    rr["mm4"] += 1
                    ps, pb = pf[:, sidx, :], bank[sidx]
                    pe_group(PE, [lambda: nc.tensor.matmul(
                        ps.rearrange("p (g q) -> p g q", g=4),
                        lhsT=kT[hf * 64:(hf + 1) * 64, pr, kb * 128:(kb + 1) * 128],
                        rhs=qT[hf * 64:(hf + 1) * 64, pr * 4:pr * 4 + 4, m * 128:(m + 1) * 128],
                        start=True, stop=True)], reads=[b_kT, b_qT], writes=[pb])
                    op(ACT, lambda ch=ch, kvh=kvh, ps=ps: nc.scalar.activation(
                        out=pT[:, ch, kvh, :], in_=ps, func=AF.Exp, scale=HD ** -0.5),
                       reads=[pb], writes=[b_PH[slot]])
            for ch in chunks:
                op(DVE, lambda ch=ch: nc.vector.tensor_tensor(
                    out=pT[:, ch, :, :].rearrange("p kv (g q) -> p (kv g) q", g=4),
                    in0=pT[:, ch, :, :].rearrange("p kv (g q) -> p (kv g) q", g=4),
                    in1=masks[:, ch, :].unsqueeze(1).to_broadcast([128, 16, 128]), op=ALU.mult),
                   reads=[b_PH[slot], b_const], writes=[b_PH[slot]])
            return chunks

        def pv_block(m, chunks):
            slot = m % 2
            pT = PH[:, slot * 4096:(slot + 1) * 4096].rearrange("p (ch kv n) -> p ch kv n", ch=2, kv=4)
            for pr in range(2):
                if pr == 0:
                    Ob, Db, obuf = pf[:, 4, :], pf[:, 5, :], [bank[4], bank[5]]
                else:
                    Ob, Db, obuf = trf[:, 0, :], trf[:, 1, :], [b_tr]
                fns = []
                for hf in range(2):
                    kvh = 2 * pr + hf
                    for i, ch in enumerate(chunks):
                        kb = m + ch
                        fns.append(lambda hf=hf, kvh=kvh, ch=ch, kb=kb, i=i, Ob=Ob: nc.tensor.matmul(
                            Ob[hf * 64:(hf + 1) * 64, :], lhsT=vB[:, kb, kvh * 64:(kvh + 1) * 64],
                            rhs=pT[:, ch, kvh, :], start=(i == 0), stop=(i == len(chunks) - 1)))
                        fns.append(lambda hf=hf, kvh=kvh, ch=ch, i=i, Db=Db: nc.tensor.matmul(
                            Db[hf * 64:(hf + 1) * 64, :], lhsT=ones_b[:, :],
                            rhs=pT[:, ch, kvh, :], start=(i == 0), stop=(i == len(chunks) - 1)))
                pe_group(PE, fns, reads=[b_vB, b_PH[slot], b_const], writes=obuf)
                dn, dnb = get_t32()
                op(DVE, lambda pr=pr, dn=dn, Db=Db: nc.vector.tensor_tensor(
                    out=dn, in0=Db, in1=ES[:, pr, :, :].rearrange("p g q -> p (g q)"), op=ALU.add),
                   reads=obuf + [b_const], writes=[dnb])
                op(DVE, lambda dn=dn: nc.vector.reciprocal(dn, dn), reads=[dnb], writes=[dnb])
                op(DVE, lambda pr=pr, dn=dn, Ob=Ob: nc.vector.tensor_tensor(
                    out=MH[:, pr * 4:pr * 4 + 4, m * 128:(m + 1) * 128],
                    in0=Ob.rearrange("p (g q) -> p g q", g=4),
                    in1=dn.rearrange("p (g q) -> p g q", g=4), op=ALU.mult),
                   reads=obuf + [dnb], writes=[b_MHa[m]])

        if ti == 0:
            gen_diag()

        prev = None
        for m in range(NBK):
            chs = qk_block(m)
            if prev is not None:
                pv_block(*prev)
            ag_groups(m)
            prev = (m, chs)
        pv_block(*prev)

        conv_pe(0)
        conv_pe(1)
        conv_pe(2)
        conv_pe(3)
        mean, mb_ = t32[:, NT32 - 2, :], b_t32[NT32 - 2]
        var, vb_ = t32[:, NT32 - 1, :], b_t32[NT32 - 1]
        op(ACT, lambda: nc.scalar.copy(mean, pf[:, 4, :]), reads=[bank[4]], writes=[mb_])
        op(ACT, lambda: nc.scalar.copy(var, pf[:, 5, :]), reads=[bank[5]], writes=[vb_])
        msq, msb = get_t32()
        op(DVE, lambda: nc.vector.tensor_tensor(out=msq, in0=mean, in1=mean, op=ALU.mult), reads=[mb_],
           writes=[msb])
        op(DVE, lambda: nc.vector.tensor_tensor(out=var, in0=var, in1=msq, op=ALU.subtract),
           reads=[msb, vb_], writes=[vb_])
        op(ACT, lambda: nc.scalar.activation(out=var, in_=var, func=AF.Sqrt, bias=eps_c[:, :]),
           reads=[vb_, b_const], writes=[vb_])
        op(DVE, lambda: nc.vector.reciprocal(var, var), reads=[vb_], writes=[vb_])

        def ln_chunk(c):
            t1, t1b = get_t32()
            op(DVE, lambda: nc.vector.tensor_tensor(out=t1, in0=ybf[:, c, :], in1=mean, op=ALU.subtract),
               reads=[b_yc[c], mb_], writes=[t1b])
            op(DVE, lambda: nc.vector.tensor_tensor(out=t1, in0=t1, in1=var, op=ALU.mult),
               reads=[t1b, vb_], writes=[t1b])
            op(ACT, lambda: nc.scalar.activation(out=MH[:, 8 + c, :], in_=t1, func=AF.Silu,
                                                 bias=col("bln", c), scale=col("gln", c)),
               reads=[t1b, b_const], writes=b_MHcb)
        for c in range(8):
            ln_chunk(c)
        if last:
            pe_group(PE, [(lambda c=c: nc.tensor.transpose(pf[0:CB, 2 + c // 4, (c % 4) * 128:(c % 4 + 1) * 128],
                                                            uhalo[:, c, :], ident_f[:, :])) for c in range(8)],
                     reads=[b_uh, b_const], writes=[bank[2], bank[3]])
            for hh in range(2):
                og, ob = get_ostg()
                op(ACT, lambda og=og, hh=hh: nc.scalar.copy(og[0:CB, :], pf[0:CB, 2 + hh, :]), reads=[bank[2 + hh]],
                   writes=[ob])
                dma(SP, osm[rr['osl']], ncp[seq][:, hh * 512:(hh + 1) * 512], og[0:CB, :], reads=[ob])


        for n in range(4):
            Wp, bW = panel(gbase + P_WO + n)
            if n == 0:
                for m in range(NBK):
                    pe_group(PE, [(lambda kc=kc: nc.tensor.matmul(pf[:, m, :], lhsT=MH[:, kc, m * 128:(m + 1) * 128],
                                                                  rhs=Wp[:, kc, :], start=(kc == 0), stop=False))
                                  for kc in range(8)], reads=[bW, b_MHa[m]], writes=[bank[m]])
                for m in range(NBK):
                    pe_group(PE, [(lambda kc=kc: nc.tensor.matmul(pf[:, m, :], lhsT=MH[:, kc, m * 128:(m + 1) * 128],
                                                                  rhs=Wp[:, kc, :], start=False, stop=(kc == 15)))
                                  for kc in range(8, 16)], reads=[bW, b_MHcb[m]], writes=[bank[m]])
                    op(DVE, lambda m=m: nc.vector.tensor_tensor(
                        out=x1[:, m, 0:512], in0=pf[:, m, :], in1=x1[:, m, 0:512], op=ALU.add),
                       reads=[bank[m]], writes=[b_x1[m]])
                continue
            for m in range(NBK):
                ps, pb = get_mm()
                pe_group(PE, [(lambda kc=kc: nc.tensor.matmul(ps, lhsT=MH[:, kc, m * 128:(m + 1) * 128],
                                                              rhs=Wp[:, kc, :], start=(kc == 0), stop=(kc == 15)))
                              for kc in range(16)], reads=[bW, b_MHa[m], b_MHcb[m]], writes=[pb])
                op(DVE, lambda m=m, n=n, ps=ps: nc.vector.tensor_tensor(
                    out=x1[:, m, n * 512:(n + 1) * 512], in0=ps, in1=x1[:, m, n * 512:(n + 1) * 512], op=ALU.add),
                   reads=[pb], writes=[b_x1[m]])
                if n == 3:
                    if m >= 1:
                        nt_back(128, MH, [b_MHa[m - 1], b_MHcb[m - 1]], (m - 1) * 128, "gmlp")
                    nt_front(x1[:, m, :], b_x1[m], 128)
        nt_back(128, MH, [b_MHa[NBK - 1], b_MHcb[NBK - 1]], (NBK - 1) * 128, "gmlp")

        hff = PH[:, :].rearrange("p (f t) -> p f t", t=T)
        for qt in range(4):
            for i in range(4):
                if qt >= 2 and next_a:
                    next_a.pop(0)()
                Wp, bW = panel(gbase + P_FF + qt * 8 + i)
                for sub in range(4):
                    ps, pb = get_mm()
                    pe_group(PE, [(lambda kc=kc: nc.tensor.matmul(ps, lhsT=Wp[:, kc, sub * 128:(sub + 1) * 128],
                                                                  rhs=MH[:, kc, :], start=(kc == 0),
                                                                  stop=(kc == 15))) for kc in range(16)],
                             reads=[bW] + b_MHb, writes=[pb])
                    r_, rb = get_t32()
                    op(ACT, lambda ps=ps, r_=r_: nc.scalar.activation(out=r_, in_=ps, func=AF.Relu), reads=[pb],
                       writes=[rb])
                    f = i * 4 + sub
                    op(ACT, lambda r_=r_, f=f: nc.scalar.activation(out=hff[:, f, :], in_=r_, func=AF.Square),
                       reads=[rb], writes=b_PH)
            for n in range(4):
                Wp, bW = panel(gbase + P_FF + qt * 8 + 4 + n)
                for m in range(NBK):
                    ps, pb = pf[:, 2 + m, :], bank[2 + m]
                    pe_group(PE, [(lambda kc=kc: nc.tensor.matmul(ps, lhsT=hff[:, kc, m * 128:(m + 1) * 128],
                                                                  rhs=Wp[:, kc, :], start=(kc == 0),
                                                                  stop=(kc == 15))) for kc in range(16)],
                             reads=[bW] + b_PH, writes=[pb])
                    op(DVE, lambda m=m, n=n, ps=ps: nc.vector.tensor_tensor(
                        out=x1[:, m, n * 512:(n + 1) * 512], in0=ps, in1=x1[:, m, n * 512:(n + 1) * 512],
                        op=ALU.add), reads=[pb], writes=[b_x1[m]])
        for m in range(NBK):
            dma(SP, ysem[m], yp[row0 + m * 128:row0 + (m + 1) * 128, :], x1[:, m, :], reads=[b_x1[m]])

    def sample_phase_a_steps():
        def front():
            dma(SP, xsem, xin[0:NS_TOK, :], xs[:, :], writes=[b_xin])
            nt_front(xin[0:NS_TOK, :], b_xin, NS_TOK)

        def back():
            nt_back(NS_TOK, XT, b_XT, 0, "gmix")
        return [front, back, lambda: None, lambda: None, lambda: None]

    def sample_tile(gbase, pre_a_done=False):
        NTK = NS_TOK
        uS = UY[:, :].rearrange("p (c b r) -> p c b r", c=8, r=CB + SD)
        x1b2 = x1[:, 2, :].bitcast(BF16)
        pTc = x1b2[:, 0:1024]
        pTn = x1b2[0:64, 1024:2048].rearrange("p (h a n) -> p h a n", a=2, h=2)
        ckst = x1[:, 3, :].rearrange("p (b c) -> p b c", c=KVW)
        kcT = PH[:, 0:4096].rearrange("p (b a k) -> p b a k", b=NSMP, a=2)
        cvB = PH[:, 4096:8192].rearrange("p (b c) -> p b c", c=KVW)
        b_scr2, b_scr3 = b_x1[2], b_x1[3]

        dma(SP, osm_misc, nks[:, 0:128 - SD, :], ck[:, SD:128, :])
        dma(SP, osm_misc, nvs[:, 0:128 - SD, :], cv[:, SD:128, :])
        dma(SP, osm_misc, ncs[:, 0:CB - SD, :], sc[:, SD:CB, :])

        if not pre_a_done:
            for st_ in sample_phase_a_steps():
                st_()
        dma(SP, xrsem[0], x1[0:NTK, 0, :], xs[:, :], writes=[b_x1[0]])

        for g4 in range(4):
            dma(SP, xsem, xin[0:120, 0:CCH], sc[4 * g4:4 * g4 + 4].rearrange("b r c -> (b r) c"), writes=[b_xin])
            pe_group(PE, [(lambda c=c: nc.tensor.transpose(pf[:, 2 + c // 4, (c % 4) * 120:(c % 4 + 1) * 120],
                                                            xin[0:120, c * 128:(c + 1) * 128],
                                                            ident_f[0:120, 0:120])) for c in range(8)],
                     reads=[b_xin, b_const], writes=[bank[2], bank[3]])
            for h in range(2):
                op(ACT, lambda h=h, g4=g4: nc.scalar.copy(
                    uS[:, 4 * h:4 * h + 4, 4 * g4:4 * g4 + 4, 0:CB],
                    pf[:, 2 + h, 0:480].rearrange("p (c b r) -> p c b r", c=4, b=4)),
                   reads=[bank[2 + h]], writes=b_u)

        if _DBG['stop'] <= 1:
            return
        for h in range(2):
            dma(SP, xrsem[3], ckst, ck[8 * h:8 * h + 8].rearrange("b k c -> k b c"), writes=[b_scr3])
            for bp in range(4):
                bk = bank[2 + bp % 2]
                fns = []
                for bb in range(2):
                    for pr in range(2):
                        fns.append(lambda bb=bb, pr=pr, bp=bp: nc.tensor.transpose(
                            pf[:, 2 + bp % 2, (bb * 2 + pr) * 128:(bb * 2 + pr + 1) * 128],
                            ckst[:, bp * 2 + bb, pr * 128:(pr + 1) * 128], ident_f[:, :]))
                pe_group(PE, fns, reads=[b_scr3, b_const], writes=[bk])
                b0 = 8 * h + bp * 2
                op(ACT, lambda bp=bp, b0=b0: nc.scalar.copy(
                    kcT[:, b0:b0 + 2, :, :], pf[:, 2 + bp % 2, :].rearrange("p (b a k) -> p b a k", b=2, a=2)),
                   reads=[bk], writes=b_PH)
        for h in range(2):
            dma(SP, xrsem[3], ckst, cv[8 * h:8 * h + 8].rearrange("b k c -> k b c"), writes=[b_scr3])
            op(ACT, lambda h=h: nc.scalar.copy(cvB[:, 8 * h:8 * h + 8, :], ckst), reads=[b_scr3], writes=b_PH)

        if _DBG['stop'] <= 2:
            return
        def wgroup_s(Wp, bW, sub, ps, pbuf):
            pe_group(PE, [(lambda kc=kc: nc.tensor.matmul(ps[:, 0:NTK], lhsT=Wp[:, kc, sub * 128:(sub + 1) * 128],
                                                          rhs=XT[:, kc, 0:NTK], start=(kc == 0), stop=(kc == 15)))
                          for kc in range(16)], reads=[bW, b_XT], writes=[pbuf])

        def conv_chunk_s(c):
            acc_, ab = get_t32()
            acc = acc_[:, 0:NTK].rearrange("p (b i) -> p b i", i=SD)
            op(DVE, lambda: nc.vector.tensor_scalar(out=acc, in0=uS[:, c, :, 0:SD], scalar1=wdwc[:, c:c + 1],
                                                    scalar2=None, op0=ALU.mult),
               reads=[b_u[c], b_const], writes=[ab])
            for jj in range(1, CW - 1):
                op(DVE, lambda jj=jj: nc.vector.scalar_tensor_tensor(
                    out=acc, in0=uS[:, c, :, jj:jj + SD], scalar=wdwc[:, jj * 8 + c:jj * 8 + c + 1], in1=acc,
                    op0=ALU.mult, op1=ALU.add), reads=[b_u[c], ab], writes=[ab])
            jj = CW - 1
            op(DVE, lambda: nc.vector.scalar_tensor_tensor(
                out=x1[:, 1, c * NTK:(c + 1) * NTK].rearrange("p (b i) -> p b i", i=SD),
                in0=uS[:, c, :, jj:jj + SD], scalar=wdwc[:, jj * 8 + c:jj * 8 + c + 1],
                in1=acc, op0=ALU.mult, op1=ALU.add), reads=[ab, b_u[c]], writes=[b_x1[1]])

        if _DBG['stop'] <= 3:
            return
        def qk_norm_s(ps, pb, dst_ap, dst_buf, wcol, kf_ap=None):
            zs, zb = get_t32()
            sq, qb = get_t16()
            op(ACT, lambda: nc.scalar.activation(out=sq[:, 0:NTK], in_=ps[:, 0:NTK], func=AF.Square), reads=[pb],
               writes=[qb])
            op(ACT, lambda: nc.scalar.copy(zs[:, 0:NTK], ps[:, 0:NTK]), reads=[pb], writes=[zb])
            si = 2 + (rr["mm"] % 2)
            ss, ssb = pf[:, si, 0:NTK], bank[si]
            pe_group(PE, [lambda: nc.tensor.matmul(ss, lhsT=blk1[:, :], rhs=sq[:, 0:NTK], start=True, stop=True)],
                     reads=[qb, b_const], writes=[ssb])
            sd_, sdb = get_t32()
            op(ACT, lambda: nc.scalar.activation(out=sd_[:, 0:NTK], in_=ss, func=AF.Sqrt, bias=eps_c[:, :]),
               reads=[ssb, b_const], writes=[sdb])
            op(DVE, lambda: nc.vector.reciprocal(sd_[:, 0:NTK], sd_[:, 0:NTK]), reads=[sdb], writes=[sdb])
            op(DVE, lambda: nc.vector.scalar_tensor_tensor(out=dst_ap, in0=zs[:, 0:NTK], scalar=wcol,
                                                           in1=sd_[:, 0:NTK], op0=ALU.mult, op1=ALU.mult),
               reads=[zb, sdb, b_const], writes=[dst_buf])
            if kf_ap is not None:
                op(DVE, lambda: nc.vector.scalar_tensor_tensor(out=kf_ap, in0=zs[:, 0:NTK], scalar=wcol,
                                                               in1=sd_[:, 0:NTK], op0=ALU.mult, op1=ALU.mult),
                   reads=[zb, sdb, b_const], writes=[b_kf])

        for pi in range(2):
            Wp, bW = panel(gbase + P_Q + pi)
            for sub in range(4):
                c = pi * 4 + sub
                ps, pb = get_mm()
                wgroup_s(Wp, bW, sub, ps, pb)
                qk_norm_s(ps, pb, qT[:, c, 0:NTK], b_qT, col("qn"))
        Wp, bW = panel(gbase + P_KV)
        for sub in range(2):
            ps, pb = get_mm()
            wgroup_s(Wp, bW, sub, ps, pb)
            qk_norm_s(ps, pb, kT[:, sub, 0:NTK], b_kT, col("kn"), kf_ap=kf[:, sub, 0:NTK])
        ps, pb = get_mm()
        pe_group(PE, [(lambda kc=kc: nc.tensor.matmul(ps[0:NTK, 0:KVW], lhsT=XT[:, kc, 0:NTK],
                                                      rhs=Wp[:, kc, 256:512], start=(kc == 0), stop=(kc == 15)))
                      for kc in range(16)], reads=[bW, b_XT], writes=[pb])
        op(ACT, lambda ps=ps: nc.scalar.copy(vB[0:NTK, 0, :], ps[0:NTK, 0:KVW]), reads=[pb], writes=[b_vB])
        og, ob = get_ostg()
        op(ACT, lambda ps=ps, og=og: nc.scalar.copy(og[0:NTK, 0:KVW], ps[0:NTK, 0:KVW]), reads=[pb], writes=[ob])
        for b in range(NSMP):
            dma(SP, osm[rr['osl']], nvs[b, 128 - SD:128, :], og[b * SD:(b + 1) * SD, 0:KVW], reads=[ob])
        if _DBG['stop'] <= 4:
            return
        pe_group(PE, [(lambda s2=s2: nc.tensor.transpose(pf[0:NTK, 5, s2 * 128:(s2 + 1) * 128], kf[:, s2, 0:NTK],
                                                          ident_f[:, :])) for s2 in range(2)],
                 reads=[b_kf, b_const], writes=[bank[5]])
        og, ob = get_ostg()
        op(ACT, lambda og=og: nc.scalar.copy(og[0:NTK, 0:KVW], pf[0:NTK, 5, 0:KVW]), reads=[bank[5]], writes=[ob])
        for b in range(NSMP):
            dma(SP, osm[rr['osl']], nks[b, 128 - SD:128, :], og[b * SD:(b + 1) * SD, 0:KVW], reads=[ob])
        for i in range(4):
            Wp, bW = panel(gbase + P_AG[i])
            for s2 in range(2):
                c = 2 * i + s2
                ps, pb = get_mm()
                wgroup_s(Wp, bW, 2 * s2, ps, pb)
                op(ACT, lambda c=c, ps=ps: nc.scalar.copy(uS[:, c, :, CB:CB + SD],
                                                          ps[:, 0:NTK].rearrange("p (b i) -> p b i", i=SD)),
                   reads=[pb], writes=[b_u[c]])
                ps, pb = get_mm()
                wgroup_s(Wp, bW, 2 * s2 + 1, ps, pb)
                sg, sgb = get_t32()
                op(ACT, lambda ps=ps, sg=sg: nc.scalar.activation(out=sg[:, 0:NTK], in_=ps[:, 0:NTK],
                                                                  func=AF.Sigmoid), reads=[pb], writes=[sgb])
                op(POOL, lambda c=c, sg=sg: nc.gpsimd.tensor_tensor(
                    out=uS[:, c, :, CB:CB + SD], in0=uS[:, c, :, CB:CB + SD],
                    in1=sg[:, 0:NTK].rearrange("p (b i) -> p b i", i=SD), op=ALU.mult),
                   reads=[sgb], writes=[b_u[c]])
                conv_chunk_s(c)
        un_, unb = get_t32()
        op(POOL, lambda: nc.gpsimd.tensor_copy(un_.rearrange("p (c b i) -> p c b i", c=8, i=SD),
                                               uS[:, :, :, CB:CB + SD]), reads=b_u, writes=[unb])
        for hh in range(2):
            pe_group(PE, [(lambda c4=c4: nc.tensor.transpose(
                pf[0:NTK, 2 + hh, c4 * 128:(c4 + 1) * 128], un_[:, (hh * 4 + c4) * NTK:(hh * 4 + c4 + 1) * NTK],
                ident_f[:, :])) for c4 in range(4)], reads=[unb, b_const], writes=[bank[2 + hh]])
            og, ob = get_ostg()
            op(ACT, lambda og=og, hh=hh: nc.scalar.copy(og[0:NTK, :], pf[0:NTK, 2 + hh, :]), reads=[bank[2 + hh]],
               writes=[ob])
            for b in range(NSMP):
                dma(SP, osm[rr['osl']], ncs[b, CB - SD:CB, hh * 512:(hh + 1) * 512], og[b * SD:(b + 1) * SD, :],
                    reads=[ob])

        if _DBG['stop'] <= 5:
            return
        pTb = [x1b2[:, 0:1024].rearrange("p (h a n) -> p h a n", h=2, a=2),
               x1b2[:, 2048:3072].rearrange("p (h a n) -> p h a n", h=2, a=2)]
        pe_group(PE, [(lambda hf=hf, pr=pr: nc.tensor.matmul(
            pf[0:NTK, hf, pr * 256:(pr + 1) * 256].rearrange("p (g q) -> p g q", g=4),
            lhsT=kT[hf * 64:(hf + 1) * 64, pr, 0:NTK],
            rhs=qT[hf * 64:(hf + 1) * 64, pr * 4:pr * 4 + 4, 0:NTK], start=True, stop=True))
            for pr in range(2) for hf in range(2)], reads=[b_kT, b_qT], writes=[bank[0], bank[1]])
        for hf in range(2):
            op(ACT, lambda hf=hf: nc.scalar.activation(
                out=pTn[:, hf, :, :], in_=pf[0:NTK, hf, :].rearrange("p (a n) -> p a n", a=2), func=AF.Exp,
                scale=HD ** -0.5), reads=[bank[hf]], writes=[b_scr2])
        op(POOL, lambda: nc.gpsimd.tensor_tensor(
            out=pTn.rearrange("p h a (g q) -> p (h a g) q", g=4), in0=pTn.rearrange("p h a (g q) -> p (h a g) q", g=4),
            in1=maskN[:, :].unsqueeze(1).to_broadcast([NTK, 16, NTK]), op=ALU.mult),
           reads=[b_scr2, b_const], writes=[b_scr2])
        fns = []
        for kvh in range(4):
            hf, pr = kvh % 2, kvh // 2
            fns.append(lambda hf=hf, pr=pr, kvh=kvh: nc.tensor.matmul(
                pf[hf * 64:(hf + 1) * 64, 4 + pr, 0:256],
                lhsT=vB[0:NTK, 0, kvh * 64:(kvh + 1) * 64], rhs=pTn[:, hf, pr, :], start=True, stop=False))
            fns.append(lambda hf=hf, pr=pr: nc.tensor.matmul(
                pf[hf * 64:(hf + 1) * 64, pr, 0:256],
                lhsT=ones_b[0:NTK, :], rhs=pTn[:, hf, pr, :], start=True, stop=False))
        pe_group(PE, fns, reads=[b_vB, b_scr2, b_const], writes=[bank[4], bank[5], bank[0], bank[1]])
        b_pTb = [bf("pTb0"), bf("pTb1")]

        def qk_s(b):
            sl = b % 2
            pe_group(PE, [(lambda hf=hf, pr=pr: nc.tensor.matmul(
                pf[:, 2 + hf, pr * 256:(pr + 1) * 256].rearrange("p (g q) -> p g q", g=4),
                lhsT=kcT[hf * 64:(hf + 1) * 64, b, pr, :],
                rhs=qT[hf * 64:(hf + 1) * 64, pr * 4:pr * 4 + 4, 0:NTK], start=True, stop=True))
                for pr in range(2) for hf in range(2)], reads=b_PH + [b_qT], writes=[bank[2], bank[3]])
            for hf in range(2):
                op(ACT, lambda hf=hf, sl=sl: nc.scalar.activation(
                    out=pTb[sl][:, hf, :, :], in_=pf[:, 2 + hf, :].rearrange("p (a n) -> p a n", a=2),
                    func=AF.Exp, scale=HD ** -0.5), reads=[bank[2 + hf]], writes=[b_pTb[sl], b_scr2])
            op(POOL, lambda b=b, sl=sl: nc.gpsimd.tensor_tensor(
                out=pTb[sl].rearrange("p h a (g q) -> p (h a g) q", g=4),
                in0=pTb[sl].rearrange("p h a (g q) -> p (h a g) q", g=4),
                in1=maskCA[:, b, :].unsqueeze(1).to_broadcast([128, 16, NTK]), op=ALU.mult),
               reads=[b_pTb[sl], b_const], writes=[b_pTb[sl]])

        def pv_s(b):
            sl = b % 2
            fns = []
            lastb = (b == NSMP - 1)
            for kvh in range(4):
                hf, pr = kvh % 2, kvh // 2
                fns.append(lambda hf=hf, pr=pr, kvh=kvh, b=b, sl=sl: nc.tensor.matmul(
                    pf[hf * 64:(hf + 1) * 64, 4 + pr, 0:256],
                    lhsT=cvB[:, b, kvh * 64:(kvh + 1) * 64], rhs=pTb[sl][:, hf, pr, :], start=False, stop=lastb))
                fns.append(lambda hf=hf, pr=pr, sl=sl: nc.tensor.matmul(
                    pf[hf * 64:(hf + 1) * 64, pr, 0:256],
                    lhsT=ones_b[:, :], rhs=pTb[sl][:, hf, pr, :], start=False, stop=lastb))
            pe_group(PE, fns, reads=[b_pTb[sl], b_const] + b_PH, writes=[bank[4], bank[5], bank[0], bank[1]])

        qk_s(0)
        for b in range(NSMP):
            if b + 1 < NSMP:
                qk_s(b + 1)
            pv_s(b)
        for pr in range(2):
            dn_, dnb = get_t32()
            dn = dn_[:, 0:256]
            op(DVE, lambda pr=pr, dn=dn: nc.vector.tensor_tensor(
                out=dn.rearrange("p (g q) -> p g q", g=4), in0=pf[:, pr, 0:256].rearrange("p (g q) -> p g q", g=4),
                in1=ES[:, pr, :, 0:NTK], op=ALU.add), reads=[bank[pr], b_const], writes=[dnb])
            op(DVE, lambda dn=dn: nc.vector.reciprocal(dn, dn), reads=[dnb], writes=[dnb])
            op(DVE, lambda pr=pr, dn=dn: nc.vector.tensor_tensor(
                out=MH[:, pr * 4:pr * 4 + 4, 0:NTK], in0=pf[:, 4 + pr, 0:256].rearrange("p (g q) -> p g q", g=4),
                in1=dn.rearrange("p (g q) -> p g q", g=4), op=ALU.mult), reads=[bank[4 + pr], dnb], writes=b_MHb)
        if _DBG['stop'] <= 6:
            return
        def yc(c):
            return x1[:, 1, c * NTK:(c + 1) * NTK]
        for c in range(8):
            yb, ybb = get_t16()
            yq, yqb = get_t16()
            op(ACT, lambda c=c, yb=yb: nc.scalar.activation(out=yb[:, 0:NTK], in_=yc(c), func=AF.Identity,
                                                           bias=col("bdw", c)), reads=[b_x1[1], b_const],
               writes=[ybb])
            op(ACT, lambda c=c, yq=yq: nc.scalar.activation(out=yq[:, 0:NTK], in_=yc(c), func=AF.Square,
                                                           bias=col("bdw", c)), reads=[b_x1[1], b_const],
               writes=[yqb])
            pe_group(PE, [lambda c=c, yb=yb: nc.tensor.matmul(pf[:, 2, 0:NTK], lhsT=onesN[:, :], rhs=yb[:, 0:NTK],
                                                              start=(c == 0), stop=(c == 7)),
                          lambda c=c, yq=yq: nc.tensor.matmul(pf[:, 3, 0:NTK], lhsT=onesN[:, :], rhs=yq[:, 0:NTK],
                                                              start=(c == 0), stop=(c == 7))],
                     reads=[ybb, yqb, b_const], writes=[bank[2], bank[3]])
        mean, mb_ = t32[:, NT32 - 2, 0:NTK], b_t32[NT32 - 2]
        var, vb_ = t32[:, NT32 - 1, 0:NTK], b_t32[NT32 - 1]
        op(ACT, lambda: nc.scalar.copy(mean, pf[:, 2, 0:NTK]), reads=[bank[2]], writes=[mb_])
        op(DVE, lambda: nc.vector.tensor_tensor(out=var, in0=mean, in1=mean, op=ALU.mult), reads=[mb_],
           writes=[vb_])
        op(DVE, lambda: nc.vector.tensor_tensor(out=var, in0=pf[:, 3, 0:NTK], in1=var, op=ALU.subtract),
           reads=[bank[3], vb_], writes=[vb_])
        op(ACT, lambda: nc.scalar.activation(out=var, in_=var, func=AF.Sqrt, bias=eps_c[:, :]),
           reads=[vb_, b_const], writes=[vb_])
        op(DVE, lambda: nc.vector.reciprocal(var, var), reads=[vb_], writes=[vb_])
        for c in range(8):
            t1_, t1b = get_t32()
            t1 = t1_[:, 0:NTK]
            op(DVE, lambda c=c, t1=t1: nc.vector.scalar_tensor_tensor(
                out=t1, in0=yc(c), scalar=col("bdw", c), in1=mean, op0=ALU.add, op1=ALU.subtract),
               reads=[b_x1[1], mb_, b_const], writes=[t1b])
            op(DVE, lambda t1=t1: nc.vector.tensor_tensor(out=t1, in0=t1, in1=var, op=ALU.mult),
               reads=[t1b, vb_], writes=[t1b])
            op(ACT, lambda c=c, t1=t1: nc.scalar.activation(out=MH[:, 8 + c, 0:NTK], in_=t1, func=AF.Silu,
                                                           bias=col("bln", c), scale=col("gln", c)),
               reads=[t1b, b_const], writes=b_MHb)

        if _DBG.get("dump"):
            dma(POOL, osm_misc, dbg_out, MH[:, :, 0:NTK], reads=b_MHb)
        if _DBG['stop'] <= 7:
            return
        for n in range(4):
            Wp, bW = panel(gbase + P_WO + n)
            ps, pb = get_mm()
            pe_group(PE, [(lambda kc=kc: nc.tensor.matmul(ps[0:NTK, :], lhsT=MH[:, kc, 0:NTK], rhs=Wp[:, kc, :],
                                                          start=(kc == 0), stop=(kc == 15))) for kc in range(16)],
                     reads=[bW] + b_MHb, writes=[pb])
            op(DVE, lambda n=n, ps=ps: nc.vector.tensor_tensor(
                out=x1[0:NTK, 0, n * 512:(n + 1) * 512], in0=ps[0:NTK, :], in1=x1[0:NTK, 0, n * 512:(n + 1) * 512],
                op=ALU.add), reads=[pb], writes=[b_x1[0]])
        norm_transpose(x1[0:NTK, 0, :], b_x1[0], NTK, MH, b_MHb, 0, "gmlp")

        hff = PH[:, :].rearrange("p (f t) -> p f t", t=T)
        for qt in range(4):
            for i in range(4):
                Wp, bW = panel(gbase + P_FF + qt * 8 + i)
                for sub in range(4):
                    ps, pb = get_mm()
                    pe_group(PE, [(lambda kc=kc: nc.tensor.matmul(ps[:, 0:NTK],
                                                                  lhsT=Wp[:, kc, sub * 128:(sub + 1) * 128],
                                                                  rhs=MH[:, kc, 0:NTK], start=(kc == 0),
                                                                  stop=(kc == 15))) for kc in range(16)],
                             reads=[bW] + b_MHb, writes=[pb])
                    r_, rb = get_t32()
                    op(ACT, lambda ps=ps, r_=r_: nc.scalar.activation(out=r_[:, 0:NTK], in_=ps[:, 0:NTK],
                                                                      func=AF.Relu), reads=[pb], writes=[rb])
                    f = i * 4 + sub
                    op(ACT, lambda r_=r_, f=f: nc.scalar.activation(out=hff[:, f, 0:NTK], in_=r_[:, 0:NTK],
                                                                    func=AF.Square), reads=[rb], writes=b_PH)
            for n in range(4):
                Wp, bW = panel(gbase + P_FF + qt * 8 + 4 + n)
                ps, pb = pf[:, 2 + n % 2, :], bank[2 + n % 2]
                pe_group(PE, [(lambda kc=kc: nc.tensor.matmul(ps[0:NTK, :], lhsT=hff[:, kc, 0:NTK],
                                                              rhs=Wp[:, kc, :], start=(kc == 0), stop=(kc == 15)))
                              for kc in range(16)], reads=[bW] + b_PH, writes=[pb])
                op(DVE, lambda n=n, ps=ps: nc.vector.tensor_tensor(
                    out=x1[0:NTK, 0, n * 512:(n + 1) * 512], in0=ps[0:NTK, :],
                    in1=x1[0:NTK, 0, n * 512:(n + 1) * 512], op=ALU.add), reads=[pb], writes=[b_x1[0]])
        dma(SP, ysem[0], ys[:, :], x1[0:NTK, 0, :], reads=[b_x1[0]])

    ntiles = NSEQ * (S // T) + 1
    wstate["total"] = ntiles * NPANEL
    ti = 0
    if _DBG["prompt"]:
        tl = [(seq, j) for seq in range(NSEQ) for j in range(S // T)]
        prompt_phase_a(*tl[0])
        for k, (seq, j) in enumerate(tl):
            if k + 1 < len(tl):
                nxt = prompt_phase_a_steps(*tl[k + 1])
            else:
                nxt = sample_phase_a_steps() if _DBG["sample"] else None
            prompt_tile(ti, seq, j, ti * NPANEL, next_a=nxt)
            assert not nxt
            ti += 1
    if _DBG["sample"]:
        sample_tile(ti * NPANEL, pre_a_done=_DBG["prompt"])

    for d in out_sems:
        if d.cnt > 0:
            nc.sync.wait_ge(d.sem, d.cnt)
    es.close()
    return nc


_NC_CACHE = {}


def kernel(x_prompt, x_sample, cache_k, cache_v, state_conv, g_mix_norm, w_in, q_norm, k_norm, sinks,
           w_dw, b_dw, g_conv_ln, b_conv_ln, w_out, g_mlp_norm, w_up, w_down):
    f = lambda a: np.ascontiguousarray(np.asarray(a, dtype=np.float32))
    x_prompt = f(x_prompt)
    x_sample = f(x_sample)
    cache_k = f(cache_k)
    cache_v = f(cache_v)
    state_conv = f(state_conv)
    shared = {
        "g_mix": f(g_mix_norm)[0], "w_in": f(w_in)[0], "q_norm": f(q_norm)[0], "k_norm": f(k_norm)[0],
        "sinks": f(sinks)[0], "w_dw": f(w_dw)[0], "b_dw": f(b_dw)[0], "g_ln": f(g_conv_ln)[0],
        "b_ln": f(b_conv_ln)[0], "w_out": f(w_out)[0], "g_mlp": f(g_mlp_norm)[0], "w_up": f(w_up)[0],
        "w_down": f(w_down)[0],
    }
    in_maps = []
    for c in range(NCORES):
        m = dict(shared)
        m["xp"] = x_prompt[NSEQ * c:NSEQ * (c + 1)].reshape(NSEQ * S, D)
        m["xs"] = x_sample[NSMP * c:NSMP * (c + 1)].reshape(NS_TOK, D)
        m["ck"] = cache_k[0, NSMP * c:NSMP * (c + 1)].reshape(NSMP, 128, KVW)
        m["cv"] = cache_v[0, NSMP * c:NSMP * (c + 1)].reshape(NSMP, 128, KVW)
        m["sc"] = state_conv[0, NSMP * c:NSMP * (c + 1)]
        in_maps.append(m)
    if "nc" not in _NC_CACHE:
        _NC_CACHE["nc"] = build_program()
    nc = _NC_CACHE["nc"]
    res = run_bass_kernel_spmd(nc, in_maps, core_ids=list(range(NCORES)))
    R = res.results
    cat = lambda k: np.concatenate([np.asarray(r[k]) for r in R], axis=0)
    y_prompt = cat("yp").reshape(16, S, D)
    y_sample = cat("ys").reshape(128, SD, D)
    nk_p = cat("nkp").reshape(1, 16, 128, 4, HD)
    nv_p = cat("nvp").reshape(1, 16, 128, 4, HD)
    nc_p = cat("ncp").reshape(1, 16, CB, CCH)
    nk_s = cat("nks").reshape(1, 128, 128, 4, HD)
    nv_s = cat("nvs").reshape(1, 128, 128, 4, HD)
    nc_s = cat("ncs").reshape(1, 128, CB, CCH)
    return (y_prompt.astype(np.float32), y_sample.astype(np.float32), nk_p.astype(np.float32),
            nv_p.astype(np.float32), nc_p.astype(np.float32), nk_s.astype(np.float32),
            nv_s.astype(np.float32), nc_s.astype(np.float32))
```
